# Optimizing a Trainium2 kernel written in Bass

```python
import math
import jax, jax.numpy as jnp
from jax import lax
import numpy as np

D_MODEL = 1024
BATCH = 8
SEQ = 8192
DEPTH = 4

CHUNK = 64
Q_BLOCK = 128
ATT_HEADS = 8
ATT_HEAD_DIM = 64
IDX_HEADS = 8
IDX_DIM = 64
TOPK_MAX = 256
ROPE_THETA = 500000.0
ROPE_FRACTION_DIV = 4
GLA_HEADS = 4
GLA_DK = 128
GLA_DV = 256
GLA_GATE_RANK = 16
GLA_TAU = 16.0
D_FF = 2816
CONV_WIDTH = 3
LN_EPS = 1e-5
DN_ALPHA = (2.0 * DEPTH) ** 0.25
DN_BETA = (8.0 * DEPTH) ** -0.25

ATT_WIDTH = ATT_HEADS * ATT_HEAD_DIM
IDXQ_WIDTH = IDX_HEADS * IDX_DIM
GLA_K_WIDTH = GLA_HEADS * GLA_DK
GLA_V_WIDTH = GLA_HEADS * GLA_DV
IN_SPLITS = (ATT_WIDTH, ATT_WIDTH, ATT_WIDTH,
             IDXQ_WIDTH, IDX_DIM, IDX_HEADS,
             GLA_K_WIDTH, GLA_K_WIDTH, GLA_V_WIDTH, GLA_GATE_RANK, GLA_V_WIDTH,
             D_MODEL, D_MODEL)
IN_WIDTH = sum(IN_SPLITS)

kernel_name = "hybrid_dsa_gla_convffn_deepnorm"

F32 = jnp.float32


def layer_norm(x, g, b):
    xf = x.astype(F32)
    mu = jnp.mean(xf, axis=-1, keepdims=True)
    var = jnp.mean(jnp.square(xf - mu), axis=-1, keepdims=True)
    return ((xf - mu) * lax.rsqrt(var + LN_EPS) * g.astype(F32) + b.astype(F32)).astype(x.dtype)


def partial_rope(x, pos):
    dh = x.shape[-1]
    rd = dh // ROPE_FRACTION_DIV
    half = rd // 2
    inv = ROPE_THETA ** (-jnp.arange(half, dtype=F32) * 2.0 / rd)
    ang = pos.astype(F32)[..., None] * inv
    cos = jnp.cos(ang)[:, :, None, :]
    sin = jnp.sin(ang)[:, :, None, :]
    xr = x[..., :rd].astype(F32)
    x1, x2 = xr[..., :half], xr[..., half:]
    rot = jnp.concatenate([x1 * cos - x2 * sin, x2 * cos + x1 * sin], axis=-1)
    return jnp.concatenate([rot.astype(x.dtype), x[..., rd:]], axis=-1)


def dsa_attention(q, k, v, q_idx, k_idx, w_idx):
    B, S, H, Dh = q.shape
    topk = min(TOPK_MAX, S // 4)
    nb = S // Q_BLOCK
    key_chunk = jnp.arange(S) // CHUNK
    k_idx32 = k_idx.astype(F32)

    def to_blocks(a):
        return jnp.moveaxis(a.reshape((B, nb, Q_BLOCK) + a.shape[2:]), 1, 0)

    def block(args):
        qb, qib, wb, start = args
        q_chunk = (start + jnp.arange(Q_BLOCK)) // CHUNK
        admissible = key_chunk[None, :] <= q_chunk[:, None]
        dots = jnp.einsum('bqhd,bsd->bqhs', qib.astype(F32), k_idx32) * (IDX_DIM ** -0.5)
        score = jnp.einsum('bqhs,bqh->bqs', jax.nn.relu(dots),
                           wb.astype(F32) * (IDX_HEADS ** -0.5))
        score = jnp.where(admissible[None], score, -jnp.inf)
        top_val, top_idx = lax.top_k(score, topk)
        valid = jnp.isfinite(top_val)
        k_sel = jax.vmap(lambda kk, ii: kk[ii])(k, top_idx)
        v_sel = jax.vmap(lambda vv, ii: vv[ii])(v, top_idx)
        logits = jnp.einsum('bqhd,bqkhd->bhqk', qb.astype(F32), k_sel.astype(F32)) * (Dh ** -0.5)
        logits = jnp.where(valid[:, None], logits, -jnp.inf)
        p = jax.nn.softmax(logits, axis=-1)
        return jnp.einsum('bhqk,bqkhd->bqhd', p, v_sel.astype(F32)).astype(q.dtype)

    starts = jnp.arange(nb) * Q_BLOCK
    out = lax.map(block, (to_blocks(q), to_blocks(q_idx), to_blocks(w_idx), starts))
    return jnp.moveaxis(out, 0, 1).reshape(B, S, H, Dh)


def gla_chunked(q, k, v, log_a):
    B, S, H, Dk = q.shape
    Dv = v.shape[-1]
    nc = S // CHUNK

    def chunks(a):
        return jnp.moveaxis(a.astype(F32).reshape((B, nc, CHUNK) + a.shape[2:]), 1, 0)

    def step(state, inp):
        qc, kc, vc, lac = inp
        cum = jnp.cumsum(lac, axis=1)
        total = cum[:, -1]
        k_dec = kc * jnp.exp(total[:, None] - cum)
        state = jnp.exp(total)[..., None] * state + jnp.einsum('bchk,bchv->bhkv', k_dec, vc)
        out = jnp.einsum('bchk,bhkv->bchv', qc, state)
        return state, out

    s0 = jnp.zeros((B, H, Dk, Dv), F32)
    _, o = lax.scan(step, s0, (chunks(q) * (Dk ** -0.5), chunks(k), chunks(v), chunks(log_a)))
    return jnp.moveaxis(o, 0, 1).reshape(B, S, H, Dv)


def hybrid_mixer(x, pos, w_in, gla_w_gate, gla_b_gate, gla_norm_g, p_attn, p_gla, w_out, b_out):
    B, S, _ = x.shape
    proj = x @ w_in
    offsets = np.cumsum(IN_SPLITS)[:-1].tolist()
    (aq, ak, av, iq, ik, iw, gq, gk, gv, g_lr, g_r, gate_a, gate_b) = jnp.split(proj, offsets, axis=-1)

    aq = partial_rope(aq.reshape(B, S, ATT_HEADS, ATT_HEAD_DIM), pos)
    ak = partial_rope(ak.reshape(B, S, ATT_HEADS, ATT_HEAD_DIM), pos)
    av = av.reshape(B, S, ATT_HEADS, ATT_HEAD_DIM)
    iq = partial_rope(iq.reshape(B, S, IDX_HEADS, IDX_DIM), pos)
    ik = partial_rope(ik[:, :, None, :], pos)[:, :, 0, :]
    ya = dsa_attention(aq, ak, av, iq, ik, iw).reshape(B, S, ATT_WIDTH) @ p_attn

    gate_logits = g_lr @ gla_w_gate + gla_b_gate
    log_a = jax.nn.log_sigmoid(gate_logits.astype(F32)) / GLA_TAU
    o = gla_chunked(gq.reshape(B, S, GLA_HEADS, GLA_DK),
                    gk.reshape(B, S, GLA_HEADS, GLA_DK),
                    gv.reshape(B, S, GLA_HEADS, GLA_DV),
                    log_a.reshape(B, S, GLA_HEADS, GLA_DK))
    mu = jnp.mean(o, axis=-1, keepdims=True)
    var = jnp.mean(jnp.square(o - mu), axis=-1, keepdims=True)
    o = (o - mu) * lax.rsqrt(var + LN_EPS) * gla_norm_g.astype(F32).reshape(GLA_HEADS, GLA_DV)
    o = o.reshape(B, S, GLA_V_WIDTH).astype(x.dtype) * jax.nn.silu(g_r)
    yb = o @ p_gla

    merged = jax.nn.sigmoid(gate_a) * ya + jax.nn.sigmoid(gate_b) * yb
    return merged @ w_out + b_out


def conv_ffn(x, w_up, conv_w, conv_b, w_down):
    u = x @ w_up
    c = u.shape[-1]
    u = lax.conv_general_dilated(u, conv_w[:, None, :], window_strides=(1,),
                                 padding=[(CONV_WIDTH - 1, 0)],
                                 dimension_numbers=('NWC', 'WIO', 'NWC'),
                                 feature_group_count=c) + conv_b
    a, b = jnp.split(u, 2, axis=-1)
    return (jax.nn.gelu(a) * b) @ w_down


def setup_inputs(seed: int = 0) -> dict:
    key = jax.random.key(seed)
    ks = jax.random.split(key, 20)
    L, D = DEPTH, D_MODEL
    nrm = lambda k, shape, s: jax.random.normal(k, shape, F32) * s
    x = jax.random.normal(ks[0], (BATCH, SEQ, D), F32)
    offs = jax.random.randint(ks[1], (BATCH, 1), 0, 64) * CHUNK
    positions = (offs + jnp.arange(SEQ, dtype=jnp.int32)[None, :]).astype(jnp.int32)
    return {
        "x": x,
        "positions": positions,
        "w_in": nrm(ks[2], (L, D, IN_WIDTH), D ** -0.5),
        "gla_w_gate": nrm(ks[3], (L, GLA_GATE_RANK, GLA_K_WIDTH), GLA_GATE_RANK ** -0.5),
        "gla_b_gate": nrm(ks[4], (L, GLA_K_WIDTH), 0.1) + 1.0,
        "gla_norm_g": 1.0 + nrm(ks[5], (L, GLA_V_WIDTH), 0.02),
        "p_attn": nrm(ks[6], (L, ATT_WIDTH, D), ATT_WIDTH ** -0.5),
        "p_gla": nrm(ks[7], (L, GLA_V_WIDTH, D), GLA_V_WIDTH ** -0.5),
        "w_mix_out": nrm(ks[8], (L, D, D), DN_BETA * D ** -0.5),
        "b_mix_out": nrm(ks[9], (L, D), 0.02),
        "ln1_g": 1.0 + nrm(ks[10], (L, D), 0.02),
        "ln1_b": nrm(ks[11], (L, D), 0.02),
        "w_up": nrm(ks[12], (L, D, 2 * D_FF), D ** -0.5),
        "conv_w": nrm(ks[13], (L, CONV_WIDTH, 2 * D_FF), CONV_WIDTH ** -0.5),
        "conv_b": nrm(ks[14], (L, 2 * D_FF), 0.02),
        "w_down": nrm(ks[15], (L, D_FF, D), DN_BETA * D_FF ** -0.5),
        "ln2_g": 1.0 + nrm(ks[16], (L, D), 0.02),
        "ln2_b": nrm(ks[17], (L, D), 0.02),
    }


def reference(x, positions, w_in, gla_w_gate, gla_b_gate, gla_norm_g, p_attn, p_gla,
              w_mix_out, b_mix_out, ln1_g, ln1_b, w_up, conv_w, conv_b, w_down, ln2_g, ln2_b):
    for l in range(DEPTH):
        y = hybrid_mixer(x, positions, w_in[l], gla_w_gate[l], gla_b_gate[l], gla_norm_g[l],
                         p_attn[l], p_gla[l], w_mix_out[l], b_mix_out[l])
        x = layer_norm(DN_ALPHA * x + y, ln1_g[l], ln1_b[l])
        y = conv_ffn(x, w_up[l], conv_w[l], conv_b[l], w_down[l])
        x = layer_norm(DN_ALPHA * x + y, ln2_g[l], ln2_b[l])
    return x
```

```python
import math
import numpy as np
import concourse.bass as bass
import concourse.mybir as mybir
from concourse.bass_utils import run_bass_kernel_spmd
from contextlib import ExitStack

F32 = mybir.dt.float32
BF16 = mybir.dt.bfloat16
I32 = mybir.dt.int32
AF = mybir.ActivationFunctionType
ALU = mybir.AluOpType
AX = mybir.AxisListType

D = 1024
DEPTH = 4
INW = 7256
DFF = 2816
DN_ALPHA = (2.0 * DEPTH) ** 0.25
LN_EPS = 1e-5
TOPK = 256
NBIS = 12
import os as _os
SAME_SYNC = _os.environ.get("SAME_SYNC", "1") == "1"
NEG = -1.0e30


_ALL_BUFS = []


class Buf:
    __slots__ = ("name", "w", "r")

    def __init__(self, name):
        self.name = name
        self.w = None
        self.r = {}
        _ALL_BUFS.append(self)


class Eng:
    def __init__(self, fw, key, eng, sem):
        self.fw = fw
        self.key = key
        self.eng = eng
        self.sem = sem
        self.n = 0
        self.waited = {}

    def wait(self, tok):
        if tok is None:
            return
        key, val = tok
        if self.waited.get(key, 0) >= val:
            return
        self.waited[key] = val
        self.eng.wait_ge(self.fw.semof[key], val)


class FW:
    def __init__(self, nc, es, n_dma_slots=8, same_engine_sync=SAME_SYNC):
        self.nc = nc
        self.es = es
        self.semof = {}
        self.same = same_engine_sync
        self.engs = {}
        for key, eng in (("pe", nc.tensor), ("act", nc.scalar), ("dve", nc.vector),
                         ("pool", nc.gpsimd), ("sp", nc.sync)):
            sem = es.enter_context(nc.semaphore("sem_" + key))
            self.semof[key] = sem
            self.engs[key] = Eng(self, key, eng, sem)
        self.pe, self.act, self.dve, self.pool, self.sp = (self.engs[k] for k in ("pe", "act", "dve", "pool", "sp"))
        self.slots = {}
        for q in ("sp", "pool"):
            lst = []
            for i in range(n_dma_slots):
                key = f"dma_{q}{i}"
                sem = es.enter_context(nc.semaphore(key))
                self.semof[key] = sem
                lst.append([key, 0])
            self.slots[q] = [lst, 0]
        self.ninstr = 0
        self.bsem = es.enter_context(nc.semaphore("bar_arrive"))
        self.gsem = es.enter_context(nc.semaphore("bar_go"))
        self.nbar = 0

    @staticmethod
    def _deps(reads, writes):
        deps = {}

        def add(tok):
            if tok is None:
                return
            k, v = tok
            if deps.get(k, 0) < v:
                deps[k] = v
        for b in reads:
            add(b.w)
        for b in writes:
            add(b.w)
            for k, v in b.r.items():
                add((k, v))
        return deps

    def op(self, E, fn, reads=(), writes=()):
        deps = self._deps(reads, writes)
        for k, v in deps.items():
            if k == E.key and (E.key == "pe" or not self.same):
                continue
            E.wait((k, v))
        ins = fn(E.eng)
        E.n += 1
        ins.then_inc(E.sem, 1)
        self.ninstr += 1
        tok = (E.key, E.n)
        for b in reads:
            if b.r.get(E.key, 0) < E.n:
                b.r[E.key] = E.n
        for b in writes:
            b.w = tok
            b.r = {}
        return tok

    def dma(self, Q, out, in_, reads=(), writes=(), **kw):
        lst, rr = self.slots[Q.key]
        slot = lst[rr % len(lst)]
        self.slots[Q.key][1] = rr + 1
        key, val = slot
        if val > 0:
            Q.wait((key, val))
        deps = self._deps(reads, writes)
        for k, v in deps.items():
            Q.wait((k, v))
        Q.eng.dma_start(out=out, in_=in_, **kw).then_inc(self.semof[key], 16)
        self.ninstr += 1
        slot[1] = val + 16
        tok = (key, val + 16)
        for b in reads:
            if b.r.get(key, 0) < val + 16:
                b.r[key] = val + 16
        for b in writes:
            b.w = tok
            b.r = {}
        return tok

    def barrier(self):
        toks = []
        for k, E in self.engs.items():
            if E.n > 0:
                toks.append((k, E.n))
        for q, (lst, rr) in self.slots.items():
            for key, val in lst:
                if val > 0:
                    toks.append((key, val))
        for k, E in self.engs.items():
            for tok in toks:
                if tok[0] == k:
                    continue
                E.wait(tok)
        for k, E in self.engs.items():
            if E.n > 20000:
                sem = self.es.enter_context(self.nc.semaphore(uname("sem_" + k)))
                self.semof[k] = sem
                E.sem = sem
                E.n = 0
                for E2 in self.engs.values():
                    E2.waited.pop(k, None)
        for b in _ALL_BUFS:
            b.w = None
            b.r = {}


class Ctx:
    pass


_UID = [0]


def uname(name):
    _UID[0] += 1
    return f"{name}_u{_UID[0]}"


def alloc(es, nc, name, shape, dt, n=1, psum=False):
    out = []
    for i in range(n):
        nm = uname(f"{name}{i}" if n > 1 else name)
        if psum:
            t = es.enter_context(nc.psum_tensor(nm, shape, dt))
        else:
            t = es.enter_context(nc.sbuf_tensor(nm, shape, dt))
        out.append((t, Buf(nm)))
    return out if n > 1 else out[0]


def setup_consts(C):
    fw, nc, es = C.fw, C.nc, C.es
    NT = C.NT
    C.identf, C.Bidentf = alloc(es, nc, "identf", [128, 128], F32)
    C.identb, C.Bidentb = alloc(es, nc, "identb", [128, 128], BF16)
    C.onesf, C.Bonesf = alloc(es, nc, "onesf", [128, 128], F32)
    C.cs, C.Bcs = alloc(es, nc, "cs", [128, NT, 16], F32)
    C.epsb, C.Bepsb = alloc(es, nc, "epsb", [128, 1], F32)
    C.oneb = es.enter_context(nc.sbuf_tensor("oneb", [128, 1], F32))
    fw.op(fw.pool, lambda e: e.memset(C.epsb[:], LN_EPS), writes=[C.Bepsb])
    fw.op(fw.pool, lambda e: e.memset(C.oneb[:], 1.0), writes=[C.Bepsb])
    identf, Bi = C.identf, C.Bidentf
    fw.op(fw.pool, lambda e: e.memset(identf[:], 0.0), writes=[Bi])
    fw.op(fw.pool, lambda e: e.affine_select(out=identf[:], in_=identf[:], pattern=[[-1, 128]], compare_op=ALU.not_equal,
                                             fill=1.0, base=0, channel_multiplier=1), reads=[Bi], writes=[Bi])
    fw.op(fw.pool, lambda e: e.tensor_copy(out=C.identb[:], in_=identf[:]), reads=[Bi], writes=[C.Bidentb])
    fw.op(fw.pool, lambda e: e.memset(C.onesf[:], 1.0), writes=[C.Bonesf])
    with ExitStack() as es2:
        posi, Bposi = alloc(es2, nc, "posi", [128, NT], I32)
        posf, Bposf = alloc(es2, nc, "posf", [128, NT], F32)
        R, BR = alloc(es2, nc, "ropeR", [128, NT, 16], F32)
        Ri, BRi = alloc(es2, nc, "ropeRi", [128, NT, 16], I32)
        Rk, BRk = alloc(es2, nc, "ropeRk", [128, NT, 16], F32)
        fw.dma(fw.sp, posi[:], C.pos[:, :], writes=[Bposi])
        fw.op(fw.dve, lambda e: e.tensor_copy(out=posf[:], in_=posi[:]), reads=[Bposi], writes=[Bposf])
        for j in range(8):
            cj = (500000.0 ** (-(2.0 * j) / 16.0)) / (2.0 * math.pi)
            fw.op(fw.dve, lambda e: e.tensor_scalar(out=R[:, :, j], in0=posf[:], scalar1=cj, scalar2=0.25, op0=ALU.mult, op1=ALU.add),
                  reads=[Bposf], writes=[BR])
            fw.op(fw.dve, lambda e: e.tensor_scalar(out=R[:, :, 8 + j], in0=posf[:], scalar1=cj, scalar2=None, op0=ALU.mult),
                  reads=[Bposf], writes=[BR])
        fw.op(fw.dve, lambda e: e.tensor_copy(out=Ri[:], in_=R[:]), reads=[BR], writes=[BRi])
        fw.op(fw.dve, lambda e: e.tensor_copy(out=Rk[:], in_=Ri[:]), reads=[BRi], writes=[BRk])
        fw.op(fw.dve, lambda e: e.tensor_tensor(out=R[:], in0=R[:], in1=Rk[:], op=ALU.subtract), reads=[BR, BRk], writes=[BR])
        fw.op(fw.dve, lambda e: e.tensor_scalar(out=Rk[:], in0=R[:], scalar1=0.5, scalar2=None, op0=ALU.is_gt), reads=[BR], writes=[BRk])
        fw.op(fw.dve, lambda e: e.tensor_tensor(out=R[:], in0=R[:], in1=Rk[:], op=ALU.subtract), reads=[BR, BRk], writes=[BR])
        fw.op(fw.dve, lambda e: e.tensor_scalar(out=Rk[:], in0=R[:], scalar1=-0.5, scalar2=None, op0=ALU.is_lt), reads=[BR], writes=[BRk])
        fw.op(fw.dve, lambda e: e.tensor_tensor(out=R[:], in0=R[:], in1=Rk[:], op=ALU.add), reads=[BR, BRk], writes=[BR])
        fw.op(fw.act, lambda e: e.activation(out=C.cs[:], in_=R[:], func=AF.Sin, scale=2.0 * math.pi * (1.0 - 1e-6)),
              reads=[BR], writes=[C.Bcs])
        fw.barrier()


def load_weight(C, dst, src, nchunk):
    t, B = dst
    for c in range(nchunk):
        C.fw.dma(C.fw.pool, t[:, c, :], src[c * 128:(c + 1) * 128, :], writes=[B])


def phase1(C, l, xsrc):
    fw, nc = C.fw, C.nc
    NT = C.NT
    with ExitStack() as es:
        W1 = alloc(es, nc, "w1", [128, 8, 2120], BF16)
        w1, Bw1 = W1
        load_weight(C, W1, C.w_in[l][:, 0:2120], 8)
        XT = alloc(es, nc, "p1xt", [128, 1024], F32, n=2)
        XTT = alloc(es, nc, "p1xT", [128, 8, 128], BF16, n=2)
        pr, Bpr = alloc(es, nc, "p1pr", [128, 25, 64], F32)
        tmp, Btmp = alloc(es, nc, "p1tmp", [128, 4, 25, 8], F32)
        prb, Bprb = alloc(es, nc, "p1prb", [128, 1600], BF16)
        VA = alloc(es, nc, "p1va", [128, 8, 65], BF16, n=2)
        WS = alloc(es, nc, "p1ws", [128, 8], F32, n=2)
        TT = alloc(es, nc, "p1tT", [128, 13, 128], BF16, n=2)
        pT, BpT = alloc(es, nc, "p1pT", [128, 8, 128], F32, psum=True)
        PG = alloc(es, nc, "p1pg", [128, 512], F32, n=4, psum=True)
        ptb, Bptb = alloc(es, nc, "p1ptb", [128, 16, 128], BF16, psum=True)
        for s in range(2):
            fw.op(fw.pool, lambda e: e.memset(VA[s][0][:], 1.0), writes=[VA[s][1]])
        groups = [(0, 512), (1536, 512), (512, 512), (1024, 512), (2048, 72)]
        wscale = (64.0 ** -0.5) * (8.0 ** -0.5)
        for i in range(NT):
            s = i % 2
            xt, Bxt = XT[s]
            xT, BxT = XTT[s]
            va, Bva = VA[s]
            ws, Bws = WS[s]
            tT, BtT = TT[s]
            fw.dma(fw.sp, xt[:], xsrc[i * 128:(i + 1) * 128, :], writes=[Bxt])
            for c in range(8):
                fw.op(fw.pe, lambda e: e.transpose(pT[:, c, :], xt[:, c * 128:(c + 1) * 128], C.identf[:]),
                      reads=[Bxt, C.Bidentf], writes=[BpT])
            fw.op(fw.act, lambda e: e.copy(xT[:], pT[:]), reads=[BpT], writes=[BxT])
            for gi, (c0, w) in enumerate(groups):
                pg, Bpg = PG[gi % 4]
                for c in range(8):
                    fw.op(fw.pe, lambda e: e.matmul(pg[:, 0:w], xT[:, c, :], w1[:, c, c0:c0 + w], start=(c == 0), stop=(c == 7)),
                          reads=[BxT, Bw1], writes=[Bpg])
                if gi < 3:
                    fw.op(fw.act, lambda e: e.copy(pr[:, gi * 8:(gi + 1) * 8, :], pg[:].rearrange("p (h d) -> p h d", d=64)),
                          reads=[Bpg], writes=[Bpr])
                elif gi == 3:
                    fw.op(fw.act, lambda e: e.copy(va[:, :, 0:64], pg[:].rearrange("p (h d) -> p h d", d=64)),
                          reads=[Bpg], writes=[Bva])
                else:
                    fw.op(fw.act, lambda e: e.copy(pr[:, 24, :], pg[:, 0:64]), reads=[Bpg], writes=[Bpr])
                    fw.op(fw.act, lambda e: e.mul(ws[:], pg[:, 64:72], wscale), reads=[Bpg], writes=[Bws])
            cos = C.cs[:, i:i + 1, 0:8].to_broadcast([128, 25, 8])
            sin = C.cs[:, i:i + 1, 8:16].to_broadcast([128, 25, 8])
            x1 = pr[:, :, 0:8]
            x2 = pr[:, :, 8:16]
            fw.op(fw.dve, lambda e: e.tensor_tensor(out=tmp[:, 0], in0=x1, in1=cos, op=ALU.mult), reads=[Bpr, C.Bcs], writes=[Btmp])
            fw.op(fw.dve, lambda e: e.tensor_tensor(out=tmp[:, 1], in0=x2, in1=sin, op=ALU.mult), reads=[Bpr, C.Bcs], writes=[Btmp])
            fw.op(fw.dve, lambda e: e.tensor_tensor(out=tmp[:, 2], in0=x2, in1=cos, op=ALU.mult), reads=[Bpr, C.Bcs], writes=[Btmp])
            fw.op(fw.dve, lambda e: e.tensor_tensor(out=tmp[:, 3], in0=x1, in1=sin, op=ALU.mult), reads=[Bpr, C.Bcs], writes=[Btmp])
            fw.op(fw.dve, lambda e: e.tensor_tensor(out=x1, in0=tmp[:, 0], in1=tmp[:, 1], op=ALU.subtract), reads=[Btmp], writes=[Bpr])
            fw.op(fw.dve, lambda e: e.tensor_tensor(out=x2, in0=tmp[:, 2], in1=tmp[:, 3], op=ALU.add), reads=[Btmp], writes=[Bpr])
            fw.op(fw.pool, lambda e: e.tensor_copy(out=prb[:], in_=pr[:].rearrange("p h d -> p (h d)")), reads=[Bpr], writes=[Bprb])
            for k in range(12):
                fw.op(fw.pe, lambda e: e.transpose(ptb[:, k, :], prb[:, k * 128:(k + 1) * 128], C.identb[:]),
                      reads=[Bprb, C.Bidentb], writes=[Bptb])
            fw.op(fw.pe, lambda e: e.transpose(ptb[0:64, 12, :], prb[:, 1536:1600], C.identb[:]),
                  reads=[Bprb, C.Bidentb], writes=[Bptb])
            fw.op(fw.dve, lambda e: e.tensor_copy(out=tT[:, 0:12, :], in_=ptb[:, 0:12, :]), reads=[Bptb], writes=[BtT])
            fw.op(fw.dve, lambda e: e.tensor_copy(out=tT[0:64, 12, :], in_=ptb[0:64, 12, :]), reads=[Bptb], writes=[BtT])
            tsl = slice(i * 128, (i + 1) * 128)
            fw.dma(fw.sp, C.qiT.rearrange("(c p) t -> p c t", p=128)[:, :, tsl], tT[:, 0:8, :], reads=[BtT], writes=[C.BqiT])
            fw.dma(fw.sp, C.kT.rearrange("(c p) t -> p c t", p=128)[:, :, tsl], tT[:, 8:12, :], reads=[BtT], writes=[C.BkT])
            fw.dma(fw.sp, C.ikT[:, tsl], tT[0:64, 12, :], reads=[BtT], writes=[C.BikT])
            fw.dma(fw.sp, C.va[tsl, :], va[:].rearrange("p h d -> p (h d)"), reads=[Bva], writes=[C.Bva])
            fw.dma(fw.sp, C.iw[tsl, :], ws[:], reads=[Bws], writes=[C.Biw])
            if i % 16 == 15 and i != NT - 1:
                fw.barrier()
        fw.barrier()


def phase2(C, l):
    fw, nc = C.fw, C.nc
    S, NT = C.S, C.NT
    GK = 8
    with ExitStack() as es:
        ik, Bik = alloc(es, nc, "p2ik", [64, S], BF16)
        QI = alloc(es, nc, "p2qi", [64, 16, 128], BF16, n=2)
        WJ = alloc(es, nc, "p2wj", [128, 8], F32, n=2)
        Sc, BSc = alloc(es, nc, "p2S", [128, S], F32)
        junk, Bjunk = alloc(es, nc, "p2junk", [128, S], BF16)
        RR = alloc(es, nc, "p2r", [128, 512], F32, n=2)
        bs, Bbs = alloc(es, nc, "p2bs", [128, 8], F32)
        diag, Bdiag = alloc(es, nc, "p2diag", [128, 128], F32)
        thrb, Bthrb = alloc(es, nc, "p2thrb", [128, 128], F32)
        MASK = alloc(es, nc, "p2mask", [128, NT, 128], BF16, n=2)
        KG = alloc(es, nc, "p2kg", [64, 8, GK * 128], BF16, n=2)
        VG = alloc(es, nc, "p2vg", [128, GK, 520], BF16, n=2)
        PT = alloc(es, nc, "p2pT", [128, 4, 128], BF16, n=3)
        oacc, Boacc = alloc(es, nc, "p2oacc", [128, 8, 65], F32)
        rinv, Brinv = alloc(es, nc, "p2rinv", [128, 8], F32)
        aof, Baof = alloc(es, nc, "p2aof", [128, 8, 64], F32)
        AOT = alloc(es, nc, "p2aoT", [128, 4, 128], BF16, n=2)
        PSS = alloc(es, nc, "p2pss", [128, 512], F32, n=2, psum=True)
        PST = alloc(es, nc, "p2pst", [128, 4, 128], F32, n=2, psum=True)
        PSL = alloc(es, nc, "p2psl", [128, 4, 128], F32, n=2, psum=True)
        PSO = alloc(es, nc, "p2pso", [128, 4, 65], F32, n=2, psum=True)
        fw.dma(fw.sp, ik[:], C.ikT[:, :], reads=[C.BikT], writes=[Bik])
        fw.op(fw.dve, lambda e: e.memset(bs[:], 0.0), writes=[Bbs])
        qiT_v = C.qiT.rearrange("(h d) t -> d h t", d=64)
        kT_v = C.kT.rearrange("(h d) t -> d h t", d=64)
        cn = {"s": 0, "t": 0, "l": 0, "g": 0}
        lo = bs[:, 0:1]
        w0 = bs[:, 1:2]
        mid = bs[:, 2:3]
        cnt = bs[:, 3:4]
        gw = bs[:, 4:5]
        hi = bs[:, 5:6]

        def front(j):
            N = 128 * (j + 1)
            nkt = j + 1
            s = j % 2
            qi, Bqi = QI[s]
            wj, Bwj = WJ[s]
            maskT, BmaskT = MASK[s]
            tsl = slice(j * 128, (j + 1) * 128)
            fw.dma(fw.sp, qi[:], qiT_v[:, :, tsl], reads=[C.BqiT], writes=[Bqi])
            fw.dma(fw.sp, wj[:], C.iw[tsl, :], reads=[C.Biw], writes=[Bwj])
            yield
            for kb in range((N + 511) // 512):
                cols = min(512, N - kb * 512)
                ksl = slice(kb * 512, kb * 512 + cols)
                for h in range(8):
                    pss, Bpss = PSS[cn["s"] % 2]
                    r, Br = RR[cn["s"] % 2]
                    cn["s"] += 1
                    fw.op(fw.pe, lambda e: e.matmul(pss[:, 0:cols], qi[:, 8 + h, :], ik[:, ksl], start=True, stop=True),
                          reads=[Bqi, Bik], writes=[Bpss])
                    fw.op(fw.act, lambda e: e.activation(out=r[:, 0:cols], in_=pss[:, 0:cols], func=AF.Relu),
                          reads=[Bpss], writes=[Br])
                    if h == 0:
                        fw.op(fw.dve, lambda e: e.tensor_scalar(out=Sc[:, ksl], in0=r[:, 0:cols], scalar1=wj[:, 0:1], scalar2=None,
                                                                op0=ALU.mult), reads=[Br, Bwj], writes=[BSc])
                    else:
                        fw.op(fw.dve, lambda e: e.scalar_tensor_tensor(out=Sc[:, ksl], in0=r[:, 0:cols], scalar=wj[:, h:h + 1],
                                                                       in1=Sc[:, ksl], op0=ALU.mult, op1=ALU.add),
                              reads=[Br, Bwj, BSc], writes=[BSc])
                    yield
            fw.op(fw.dve, lambda e: e.memset(Sc[0:64, N - 64:N], NEG), writes=[BSc])
            if j < 2:
                fw.op(fw.dve, lambda e: e.memset(lo, -1.0e29), writes=[Bbs])
            else:
                fw.op(fw.dve, lambda e: e.tensor_reduce(out=hi, in_=Sc[:, 0:N], axis=AX.X, op=ALU.max), reads=[BSc], writes=[Bbs])
                fw.op(fw.dve, lambda e: e.tensor_reduce(out=lo, in_=Sc[:, 0:N - 64], axis=AX.X, op=ALU.min), reads=[BSc], writes=[Bbs])
                fw.op(fw.dve, lambda e: e.tensor_tensor(out=w0, in0=hi, in1=lo, op=ALU.subtract), reads=[Bbs], writes=[Bbs])
                yield
                for it in range(1, NBIS + 1):
                    sc = 2.0 ** (-it)
                    fw.op(fw.dve, lambda e: e.scalar_tensor_tensor(out=mid, in0=w0, scalar=sc, in1=lo, op0=ALU.mult, op1=ALU.add),
                          reads=[Bbs], writes=[Bbs])

                    def f2(e):
                        e.tensor_scalar(out=junk[:, 0:N], in0=Sc[:, 0:N], scalar1=mid, scalar2=0.0, op0=ALU.is_ge, op1=ALU.add,
                                        accum_out=cnt)
                        return e.tensor_copy(out=bs[:, 6:7], in_=bs[:, 7:8])
                    fw.op(fw.dve, f2, reads=[Bbs, BSc], writes=[Bbs, Bjunk])
                    fw.op(fw.dve, lambda e: e.scalar_tensor_tensor(out=gw, in0=cnt, scalar=TOPK - 0.5, in1=w0, op0=ALU.is_ge, op1=ALU.mult),
                          reads=[Bbs], writes=[Bbs])
                    fw.op(fw.dve, lambda e: e.scalar_tensor_tensor(out=lo, in0=gw, scalar=sc, in1=lo, op0=ALU.mult, op1=ALU.add),
                          reads=[Bbs], writes=[Bbs])
                    for _ in range(max(1, N // 512)):
                        yield
            fw.op(fw.dve, lambda e: e.tensor_scalar(out=diag[:], in0=C.identf[:], scalar1=lo, scalar2=None, op0=ALU.mult),
                  reads=[Bbs, C.Bidentf], writes=[Bdiag])
            pst, Bpst = PST[cn["t"] % 2]
            cn["t"] += 1
            fw.op(fw.pe, lambda e: e.matmul(pst[:, 0, :], C.onesf[:], diag[:], start=True, stop=True),
                  reads=[Bdiag, C.Bonesf], writes=[Bpst])
            fw.op(fw.dve, lambda e: e.tensor_copy(out=thrb[:], in_=pst[:, 0, :]), reads=[Bpst], writes=[Bthrb])
            yield
            for k0 in range(0, nkt, 4):
                n4 = min(4, nkt - k0)
                pst, Bpst = PST[cn["t"] % 2]
                cn["t"] += 1
                for a in range(n4):
                    kt = k0 + a
                    fw.op(fw.pe, lambda e: e.transpose(pst[:, a, :], Sc[:, kt * 128:(kt + 1) * 128], C.identf[:]),
                          reads=[BSc, C.Bidentf], writes=[Bpst])
                fw.op(fw.dve, lambda e: e.tensor_tensor(out=maskT[:, k0:k0 + n4, :], in0=pst[:, 0:n4, :],
                                                        in1=thrb[:, None, :].to_broadcast([128, n4, 128]), op=ALU.is_ge),
                      reads=[Bpst, Bthrb], writes=[BmaskT])
                yield

        def attn(j):
            nkt = j + 1
            s = j % 2
            qi, Bqi = QI[s]
            maskT, BmaskT = MASK[s]
            tsl = slice(j * 128, (j + 1) * 128)
            ng = (nkt + GK - 1) // GK
            for g in range(ng):
                nk = min(GK, nkt - g * GK)
                kg, Bkg = KG[cn["g"] % 2]
                vg, Bvg = VG[cn["g"] % 2]
                cn["g"] += 1
                k0g = g * GK
                fw.dma(fw.sp, kg[:, :, 0:nk * 128], kT_v[:, :, k0g * 128:(k0g + nk) * 128], reads=[C.BkT], writes=[Bkg])
                fw.dma(fw.sp, vg[:, 0:nk, :], C.va[k0g * 128:(k0g + nk) * 128, :].rearrange("(k p) c -> p k c", p=128),
                       reads=[C.Bva], writes=[Bvg])
                yield
                steps = [(h, q4) for h in range(8) for q4 in range(0, nk, 4)]
                pend = None
                for (h, q4) in steps:
                    n4 = min(4, nk - q4)
                    psl, Bpsl = PSL[cn["l"] % 2]
                    pT, BpT = PT[cn["l"] % 3]
                    cn["l"] += 1
                    for a in range(n4):
                        fw.op(fw.pe, lambda e: e.matmul(psl[:, a, :], kg[:, h, (q4 + a) * 128:(q4 + a + 1) * 128], qi[:, h, :],
                                                        start=True, stop=True), reads=[Bkg, Bqi], writes=[Bpsl])
                    fw.op(fw.act, lambda e: e.activation(out=pT[:, 0:n4, :], in_=psl[:, 0:n4, :], func=AF.Exp, scale=0.125),
                          reads=[Bpsl], writes=[BpT])
                    fw.op(fw.pool, lambda e: e.tensor_tensor(out=pT[:, 0:n4, :], in0=pT[:, 0:n4, :],
                                                             in1=maskT[:, k0g + q4:k0g + q4 + n4, :], op=ALU.mult),
                          reads=[BpT, BmaskT], writes=[BpT])
                    if pend is not None:
                        ph, pq4, pn4, ppT, pBpT = pend
                        ppso, pBpso = PSO[ph // 4]
                        for a in range(pn4):
                            fw.op(fw.pe, lambda e: e.matmul(ppso[:, ph % 4, :], ppT[:, a, :], vg[:, pq4 + a, ph * 65:(ph + 1) * 65],
                                                            start=(pq4 + a == 0), stop=(pq4 + a == nk - 1)),
                                  reads=[pBpT, Bvg], writes=[pBpso])
                    pend = (h, q4, n4, pT, BpT)
                    for _ in range(n4):
                        yield
                ph, pq4, pn4, ppT, pBpT = pend
                ppso, pBpso = PSO[ph // 4]
                for a in range(pn4):
                    fw.op(fw.pe, lambda e: e.matmul(ppso[:, ph % 4, :], ppT[:, a, :], vg[:, pq4 + a, ph * 65:(ph + 1) * 65],
                                                    start=(pq4 + a == 0), stop=(pq4 + a == nk - 1)),
                          reads=[pBpT, Bvg], writes=[pBpso])
                for hb in range(2):
                    pso, Bpso = PSO[hb]
                    if g == 0:
                        fw.op(fw.dve, lambda e: e.tensor_copy(out=oacc[:, hb * 4:(hb + 1) * 4, :], in_=pso[:]), reads=[Bpso], writes=[Boacc])
                    else:
                        fw.op(fw.dve, lambda e: e.tensor_tensor(out=oacc[:, hb * 4:(hb + 1) * 4, :], in0=oacc[:, hb * 4:(hb + 1) * 4, :],
                                                                in1=pso[:], op=ALU.add), reads=[Bpso, Boacc], writes=[Boacc])
            fw.op(fw.dve, lambda e: e.reciprocal(out=rinv[:], in_=oacc[:, :, 64]), reads=[Boacc], writes=[Brinv])
            fw.op(fw.dve, lambda e: e.tensor_tensor(out=aof[:], in0=oacc[:, :, 0:64], in1=rinv[:, :, None].to_broadcast([128, 8, 64]),
                                                    op=ALU.mult), reads=[Boacc, Brinv], writes=[Baof])
            pst, Bpst = PST[cn["t"] % 2]
            cn["t"] += 1
            aoT, BaoT = AOT[s]
            aof2 = aof[:].rearrange("p h d -> p (h d)")
            for c in range(4):
                fw.op(fw.pe, lambda e: e.transpose(pst[:, c, :], aof2[:, c * 128:(c + 1) * 128], C.identf[:]),
                      reads=[Baof, C.Bidentf], writes=[Bpst])
            fw.op(fw.act, lambda e: e.copy(aoT[:], pst[:]), reads=[Bpst], writes=[BaoT])
            fw.dma(fw.sp, C.aoT.rearrange("(c p) t -> p c t", p=128)[:, :, tsl], aoT[:], reads=[BaoT], writes=[C.BaoT])
            yield

        def run_merged(gA, gF):
            la = list_len(gA[1])
            lf = list_len(gF[1]) if gF is not None else 0
            ga = gA[0]
            gf = gF[0] if gF is not None else None
            da = df = 0
            while da < la or df < lf:
                if df < lf and (da >= la or df * la <= da * lf):
                    next(gf, None)
                    df += 1
                else:
                    next(ga, None)
                    da += 1
            for g_ in (ga, gf):
                if g_ is not None:
                    for _ in g_:
                        pass

        def nyield_front(j):
            N = 128 * (j + 1)
            nkt = j + 1
            n = 1 + ((N + 511) // 512) * 8 + 1 + ((NBIS * max(1, N // 512) + 1) if j >= 2 else 0) + (nkt + 3) // 4
            return n

        def nyield_attn(j):
            nkt = j + 1
            return nkt * 8 + (nkt + GK - 1) // GK + 1

        def list_len(n):
            return n

        SEG = 32
        for j0 in range(0, NT, SEG):
            j1 = min(NT, j0 + SEG)
            for _ in front(j0):
                pass
            for j in range(j0, j1):
                gA = (attn(j), nyield_attn(j))
                gF = (front(j + 1), nyield_front(j + 1)) if j + 1 < j1 else None
                run_merged(gA, gF)
            fw.barrier()


def load_xT(C, xt, Bxt, pT, BpT, xT, BxT, src_ap):
    fw = C.fw
    fw.dma(fw.sp, xt[:], src_ap, writes=[Bxt])
    for c in range(8):
        fw.op(fw.pe, lambda e: e.transpose(pT[:, c * 128:(c + 1) * 128], xt[:, c * 128:(c + 1) * 128], C.identf[:]),
              reads=[Bxt, C.Bidentf], writes=[BpT])
    fw.op(fw.act, lambda e: e.copy(xT[:].rearrange("p c t -> p (c t)"), pT[:]), reads=[BpT], writes=[BxT])


def bcast_load(C, es, name, row_ap, n=1024):
    t, B = alloc(es, C.nc, name, [128, n], F32)
    C.fw.dma(C.fw.sp, t[:], row_ap.partition_broadcast(128), writes=[B])
    return t, B


def layer_norm_tile(C, z, Bz, outt, Bout, st6, mv, Bst, g_b, Bg, b_b, Bb):
    fw = C.fw
    for hh in range(2):
        fw.op(fw.dve, lambda e: e.bn_stats(out=st6[:, hh, :], in_=z[:, hh * 512:(hh + 1) * 512]), reads=[Bz], writes=[Bst])
    fw.op(fw.dve, lambda e: e.bn_aggr(out=mv[:, 0:2], in_=st6[:].rearrange("p a b -> p (a b)")), reads=[Bst], writes=[Bst])
    fw.op(fw.act, lambda e: e.activation(out=mv[:, 2:3], in_=mv[:, 1:2], func=AF.Ln, bias=C.epsb[:, 0:1]), reads=[Bst, C.Bepsb], writes=[Bst])
    fw.op(fw.act, lambda e: e.activation(out=mv[:, 3:4], in_=mv[:, 2:3], func=AF.Exp, scale=-0.5), reads=[Bst], writes=[Bst])
    fw.op(fw.dve, lambda e: e.tensor_scalar(out=z[:], in0=z[:], scalar1=mv[:, 0:1], scalar2=mv[:, 3:4], op0=ALU.subtract, op1=ALU.mult),
          reads=[Bz, Bst], writes=[Bz])
    fw.op(fw.dve, lambda e: e.tensor_tensor(out=z[:], in0=z[:], in1=g_b[:], op=ALU.mult), reads=[Bz, Bg], writes=[Bz])
    fw.op(fw.dve, lambda e: e.tensor_tensor(out=outt[:], in0=z[:], in1=b_b[:], op=ALU.add), reads=[Bz, Bb], writes=[Bout])


def phase3(C, l, xsrc):
    fw, nc = C.fw, C.nc
    NT = C.NT
    W0 = 2120
    with ExitStack() as es:
        WG = alloc(es, nc, "p3wg", [128, 8, 3088], BF16)
        wg, Bwg = WG
        load_weight(C, WG, C.w_in[l][:, W0:W0 + 3088], 8)
        wga, Bwga = alloc(es, nc, "p3wga", [33, 512], F32)
        fw.op(fw.pool, lambda e: e.memset(wga[:], 0.0), writes=[Bwga])
        fw.dma(fw.sp, wga[0:16, :], C.gla_w_gate[l], writes=[Bwga])
        fw.dma(fw.sp, wga[32:33, :], C.gla_b_gate[l:l + 1, :], writes=[Bwga])
        gnb, Bgnb = bcast_load(C, es, "p3gnb", C.gla_norm_g[l])
        U, BU = alloc(es, nc, "p3U", [128, 128], F32)
        fw.op(fw.pool, lambda e: e.memset(U[:], -1.0 / 16.0), writes=[BU])
        fw.op(fw.pool, lambda e: e.affine_select(out=U[:], in_=U[:], pattern=[[-1, 128]], compare_op=ALU.is_gt,
                                                 fill=0.0, base=0, channel_multiplier=1), reads=[BU], writes=[BU])
        fw.op(fw.pool, lambda e: e.memset(U[64:128, 0:64], 0.0), reads=[BU], writes=[BU])
        Ind, BInd = alloc(es, nc, "p3Ind", [128, 2], F32)
        fw.op(fw.pool, lambda e: e.memset(Ind[:], 0.0), writes=[BInd])
        fw.op(fw.pool, lambda e: e.memset(Ind[0:64, 0:1], -1.0 / 16.0), reads=[BInd], writes=[BInd])
        fw.op(fw.pool, lambda e: e.memset(Ind[64:128, 1:2], -1.0 / 16.0), reads=[BInd], writes=[BInd])
        glr, Bglr = alloc(es, nc, "p3glr", [33, 128], F32)
        fw.op(fw.pool, lambda e: e.memset(glr[:], 1.0), writes=[Bglr])
        state, Bstate = alloc(es, nc, "p3state", [128, 4, 256], F32)
        fw.op(fw.pool, lambda e: e.memset(state[:], 0.0), writes=[Bstate])
        stb, Bstb = alloc(es, nc, "p3stb", [128, 4, 256], BF16)
        XT = alloc(es, nc, "p3xt", [128, 1024], F32, n=2)
        XTT = alloc(es, nc, "p3xT", [128, 8, 128], BF16, n=2)
        gqT, BgqT = alloc(es, nc, "p3gqT", [128, 4, 128], BF16)
        gvb, Bgvb = alloc(es, nc, "p3gvb", [128, 1024], BF16)
        e1, Be1 = alloc(es, nc, "p3e1", [128, 512], F32)
        spl, Bspl = alloc(es, nc, "p3spl", [128, 512], F32)
        eD, BeD = alloc(es, nc, "p3eD", [128, 512], F32)
        kdec, Bkdec = alloc(es, nc, "p3kdec", [128, 512], BF16)
        etot, Betot = alloc(es, nc, "p3etot", [128, 8], F32)
        st6, Bst = alloc(es, nc, "p3st6", [128, 4, 6], F32)
        mv = es.enter_context(nc.sbuf_tensor(uname("p3mv"), [128, 4, 4], F32))
        on, Bon = alloc(es, nc, "p3on", [128, 1024], F32)
        sil, Bsil = alloc(es, nc, "p3sil", [128, 1024], F32)
        OT = alloc(es, nc, "p3oT", [128, 8, 128], BF16, n=2)
        P01, BP01 = alloc(es, nc, "p3P01", [128, 1024], F32, psum=True)
        P2, BP2 = alloc(es, nc, "p3P2", [128, 512], F32, psum=True)
        P3, BP3 = alloc(es, nc, "p3P3", [128, 512], F32, psum=True)
        P45, BP45 = alloc(es, nc, "p3P45", [128, 1024], F32, psum=True)
        P67, BP67 = alloc(es, nc, "p3P67", [128, 1024], F32, psum=True)
        for i in range(NT):
            s = i % 2
            xt, Bxt = XT[s]
            xT, BxT = XTT[s]
            load_xT(C, xt, Bxt, P01, BP01, xT, BxT, xsrc[i * 128:(i + 1) * 128, :])
            for h in range(4):
                for c in range(8):
                    fw.op(fw.pe, lambda e: e.matmul(P2[:, h * 128:(h + 1) * 128], wg[:, c, h * 128:(h + 1) * 128], xT[:, c, :],
                                                    start=(c == 0), stop=(c == 7)), reads=[Bwg, BxT], writes=[BP2])
            fw.op(fw.act, lambda e: e.mul(gqT[:].rearrange("p h t -> p (h t)"), P2[:], 128.0 ** -0.5), reads=[BP2], writes=[BgqT])
            for c in range(8):
                fw.op(fw.pe, lambda e: e.matmul(P3[:], xT[:, c, :], wg[:, c, 512:1024], start=(c == 0), stop=(c == 7)),
                      reads=[Bwg, BxT], writes=[BP3])
            for hb in range(2):
                for c in range(8):
                    fw.op(fw.pe, lambda e: e.matmul(P45[:, hb * 512:(hb + 1) * 512], xT[:, c, :], wg[:, c, 1024 + hb * 512:1536 + hb * 512],
                                                    start=(c == 0), stop=(c == 7)), reads=[Bwg, BxT], writes=[BP45])
            fw.op(fw.act, lambda e: e.copy(gvb[:], P45[:]), reads=[BP45], writes=[Bgvb])
            for c in range(8):
                fw.op(fw.pe, lambda e: e.matmul(P2[0:16, 0:128], wg[:, c, 2048:2064], xT[:, c, :], start=(c == 0), stop=(c == 7)),
                      reads=[Bwg, BxT], writes=[BP2])
            fw.op(fw.act, lambda e: e.copy(glr[0:16, :], P2[0:16, 0:128]), reads=[BP2], writes=[Bglr])
            fw.op(fw.pe, lambda e: e.matmul(P2[:], glr[:], wga[:], start=True, stop=True), reads=[Bglr, Bwga], writes=[BP2])
            fw.op(fw.act, lambda e: e.activation(out=e1[:], in_=P2[:], func=AF.Exp, scale=-1.0), reads=[BP2], writes=[Be1])
            fw.op(fw.act, lambda e: e.activation(out=spl[:], in_=e1[:], func=AF.Ln, bias=C.oneb[:, 0:1]), reads=[Be1, C.Bepsb], writes=[Bspl])
            fw.op(fw.pe, lambda e: e.matmul(P2[:], U[:], spl[:], start=True, stop=True), reads=[BU, Bspl], writes=[BP2])
            fw.op(fw.act, lambda e: e.activation(out=eD[:], in_=P2[:], func=AF.Exp), reads=[BP2], writes=[BeD])
            fw.op(fw.dve, lambda e: e.tensor_tensor(out=kdec[:], in0=P3[:], in1=eD[:], op=ALU.mult), reads=[BP3, BeD], writes=[Bkdec])
            for h in range(4):
                fw.op(fw.pe, lambda e: e.matmul(P2[:, h * 2:h * 2 + 2], spl[:, h * 128:(h + 1) * 128], Ind[:], start=True, stop=True),
                      reads=[Bspl, BInd], writes=[BP2])
            fw.op(fw.act, lambda e: e.activation(out=etot[:], in_=P2[:, 0:8], func=AF.Exp), reads=[BP2], writes=[Betot])
            for hb in range(2):
                for c in range(8):
                    fw.op(fw.pe, lambda e: e.matmul(P01[:, hb * 512:(hb + 1) * 512], xT[:, c, :], wg[:, c, 2064 + hb * 512:2576 + hb * 512],
                                                    start=(c == 0), stop=(c == 7)), reads=[Bwg, BxT], writes=[BP01])
            for ch in range(2):
                r0 = ch * 64
                for h in range(4):
                    fw.op(fw.pe, lambda e: e.matmul(P45[:, h * 256:(h + 1) * 256], kdec[r0:r0 + 64, h * 128:(h + 1) * 128],
                                                    gvb[r0:r0 + 64, h * 256:(h + 1) * 256], start=True, stop=True),
                          reads=[Bkdec, Bgvb], writes=[BP45])
                for h in range(4):
                    fw.op(fw.dve, lambda e: e.scalar_tensor_tensor(out=state[:, h, :], in0=state[:, h, :], scalar=etot[:, h * 2 + ch:h * 2 + ch + 1],
                                                                   in1=P45[:, h * 256:(h + 1) * 256], op0=ALU.mult, op1=ALU.add),
                          reads=[Bstate, Betot, BP45], writes=[Bstate])
                fw.op(fw.act, lambda e: e.copy(stb[:], state[:]), reads=[Bstate], writes=[Bstb])
                for h in range(4):
                    fw.op(fw.pe, lambda e: e.matmul(P67[r0:r0 + 64, h * 256:(h + 1) * 256], gqT[:, h, r0:r0 + 64], stb[:, h, :],
                                                    start=True, stop=True), reads=[BgqT, Bstb], writes=[BP67])
            for h in range(4):
                fw.op(fw.dve, lambda e: e.bn_stats(out=st6[:, h, :], in_=P67[:, h * 256:(h + 1) * 256]), reads=[BP67], writes=[Bst])
            for h in range(4):
                fw.op(fw.dve, lambda e: e.bn_aggr(out=mv[:, h, 0:2], in_=st6[:, h, :]), reads=[Bst], writes=[Bst])
            fw.op(fw.act, lambda e: e.activation(out=mv[:, :, 2], in_=mv[:, :, 1], func=AF.Ln, bias=C.epsb[:, 0:1]), reads=[Bst, C.Bepsb], writes=[Bst])
            fw.op(fw.act, lambda e: e.activation(out=mv[:, :, 3], in_=mv[:, :, 2], func=AF.Exp, scale=-0.5), reads=[Bst], writes=[Bst])
            for h in range(4):
                fw.op(fw.dve, lambda e: e.tensor_scalar(out=on[:, h * 256:(h + 1) * 256], in0=P67[:, h * 256:(h + 1) * 256],
                                                        scalar1=mv[:, h, 0:1], scalar2=mv[:, h, 3:4], op0=ALU.subtract, op1=ALU.mult),
                      reads=[BP67, Bst], writes=[Bon])
            fw.op(fw.act, lambda e: e.activation(out=sil[:], in_=P01[:], func=AF.Silu), reads=[BP01], writes=[Bsil])
            fw.op(fw.dve, lambda e: e.tensor_tensor(out=on[:], in0=on[:], in1=gnb[:], op=ALU.mult), reads=[Bon, Bgnb], writes=[Bon])
            fw.op(fw.dve, lambda e: e.tensor_tensor(out=on[:], in0=on[:], in1=sil[:], op=ALU.mult), reads=[Bon, Bsil], writes=[Bon])
            oT, BoT = OT[s]
            for c in range(8):
                fw.op(fw.pe, lambda e: e.transpose(P45[:, c * 128:(c + 1) * 128], on[:, c * 128:(c + 1) * 128], C.identf[:]),
                      reads=[Bon, C.Bidentf], writes=[BP45])
            fw.op(fw.act, lambda e: e.copy(oT[:].rearrange("p c t -> p (c t)"), P45[:]), reads=[BP45], writes=[BoT])
            fw.dma(fw.sp, C.oT.rearrange("(c p) t -> p c t", p=128)[:, :, i * 128:(i + 1) * 128], oT[:], reads=[BoT], writes=[C.BoT])
            if i % 16 == 15 and i != NT - 1:
                fw.barrier()
        fw.barrier()


def phase4(C, l, xsrc, xdst):
    fw, nc = C.fw, C.nc
    NT = C.NT
    with ExitStack() as es:
        WGT = alloc(es, nc, "p4wgt", [128, 8, 2048], BF16)
        wgt, Bwgt = WGT
        load_weight(C, WGT, C.w_in[l][:, 5208:7256], 8)
        PA = alloc(es, nc, "p4pa", [128, 4, 1024], BF16)
        load_weight(C, PA, C.p_attn[l], 4)
        PGL = alloc(es, nc, "p4pg", [128, 8, 1024], BF16)
        load_weight(C, PGL, C.p_gla[l], 8)
        WO = alloc(es, nc, "p4wo", [128, 8, 1024], BF16)
        load_weight(C, WO, C.w_mix_out[l], 8)
        pa, Bpa = PA
        pgl, Bpgl = PGL
        wo, Bwo = WO
        bob, Bbob = bcast_load(C, es, "p4bob", C.b_mix_out[l])
        lg, Blg = bcast_load(C, es, "p4lg", C.ln1_g[l])
        lb, Blb = bcast_load(C, es, "p4lb", C.ln1_b[l])
        XT = alloc(es, nc, "p4xt", [128, 1024], F32, n=2)
        XTT = alloc(es, nc, "p4xT", [128, 8, 128], BF16, n=2)
        AO = alloc(es, nc, "p4ao", [128, 4, 128], BF16, n=2)
        OTT = alloc(es, nc, "p4ot", [128, 8, 128], BF16, n=2)
        sa, Bsa = alloc(es, nc, "p4sa", [128, 1024], F32)
        sb_, Bsb = alloc(es, nc, "p4sb", [128, 1024], F32)
        mg, Bmg = alloc(es, nc, "p4mg", [128, 1024], F32)
        mT, BmT = alloc(es, nc, "p4mT", [128, 8, 128], BF16)
        ZZ = alloc(es, nc, "p4z", [128, 1024], F32, n=2)
        OUT = alloc(es, nc, "p4out", [128, 1024], F32, n=2)
        ST6 = alloc(es, nc, "p4st6", [128, 2, 6], F32, n=2)
        MV = [es.enter_context(nc.sbuf_tensor(uname("p4mv"), [128, 4], F32)) for _ in range(2)]
        P01, BP01 = alloc(es, nc, "p4P01", [128, 1024], F32, psum=True)
        P23, BP23 = alloc(es, nc, "p4P23", [128, 1024], F32, psum=True)
        P45, BP45 = alloc(es, nc, "p4P45", [128, 1024], F32, psum=True)
        P67, BP67 = alloc(es, nc, "p4P67", [128, 1024], F32, psum=True)
        for i in range(NT):
            s = i % 2
            xt, Bxt = XT[s]
            xT, BxT = XTT[s]
            ao, Bao = AO[s]
            ot, Bot = OTT[s]
            tsl = slice(i * 128, (i + 1) * 128)
            z, Bz = ZZ[s]
            st6, Bst = ST6[s]
            mv = MV[s]
            load_xT(C, xt, Bxt, P01, BP01, xT, BxT, xsrc[tsl, :])
            fw.dma(fw.sp, ao[:], C.aoT.rearrange("(c p) t -> p c t", p=128)[:, :, tsl], reads=[C.BaoT], writes=[Bao])
            fw.dma(fw.sp, ot[:], C.oT.rearrange("(c p) t -> p c t", p=128)[:, :, tsl], reads=[C.BoT], writes=[Bot])
            for gi, (dst, Bdst) in enumerate(((sa, Bsa), (sb_, Bsb))):
                for hb in range(2):
                    for c in range(8):
                        fw.op(fw.pe, lambda e: e.matmul(P23[:, hb * 512:(hb + 1) * 512], xT[:, c, :],
                                                        wgt[:, c, gi * 1024 + hb * 512:gi * 1024 + (hb + 1) * 512],
                                                        start=(c == 0), stop=(c == 7)), reads=[Bwgt, BxT], writes=[BP23])
                fw.op(fw.act, lambda e: e.activation(out=dst[:], in_=P23[:], func=AF.Sigmoid), reads=[BP23], writes=[Bdst])
            for hb in range(2):
                for c in range(4):
                    fw.op(fw.pe, lambda e: e.matmul(P45[:, hb * 512:(hb + 1) * 512], ao[:, c, :], pa[:, c, hb * 512:(hb + 1) * 512],
                                                    start=(c == 0), stop=(c == 3)), reads=[Bao, Bpa], writes=[BP45])
            for hb in range(2):
                for c in range(8):
                    fw.op(fw.pe, lambda e: e.matmul(P67[:, hb * 512:(hb + 1) * 512], ot[:, c, :], pgl[:, c, hb * 512:(hb + 1) * 512],
                                                    start=(c == 0), stop=(c == 7)), reads=[Bot, Bpgl], writes=[BP67])
            fw.op(fw.dve, lambda e: e.tensor_tensor(out=sa[:], in0=P45[:], in1=sa[:], op=ALU.mult), reads=[BP45, Bsa], writes=[Bsa])
            fw.op(fw.dve, lambda e: e.tensor_tensor(out=sb_[:], in0=P67[:], in1=sb_[:], op=ALU.mult), reads=[BP67, Bsb], writes=[Bsb])
            fw.op(fw.dve, lambda e: e.tensor_tensor(out=mg[:], in0=sa[:], in1=sb_[:], op=ALU.add), reads=[Bsa, Bsb], writes=[Bmg])
            for c in range(8):
                fw.op(fw.pe, lambda e: e.transpose(P01[:, c * 128:(c + 1) * 128], mg[:, c * 128:(c + 1) * 128], C.identf[:]),
                      reads=[Bmg, C.Bidentf], writes=[BP01])
            fw.op(fw.act, lambda e: e.copy(mT[:].rearrange("p c t -> p (c t)"), P01[:]), reads=[BP01], writes=[BmT])
            for hb in range(2):
                for c in range(8):
                    fw.op(fw.pe, lambda e: e.matmul(P45[:, hb * 512:(hb + 1) * 512], mT[:, c, :], wo[:, c, hb * 512:(hb + 1) * 512],
                                                    start=(c == 0), stop=(c == 7)), reads=[BmT, Bwo], writes=[BP45])
            fw.op(fw.dve, lambda e: e.tensor_tensor(out=z[:], in0=P45[:], in1=bob[:], op=ALU.add), reads=[BP45, Bbob], writes=[Bz])
            fw.op(fw.dve, lambda e: e.scalar_tensor_tensor(out=z[:], in0=xt[:], scalar=DN_ALPHA, in1=z[:], op0=ALU.mult, op1=ALU.add),
                  reads=[Bxt, Bz], writes=[Bz])
            o, Bo = OUT[s]
            layer_norm_tile(C, z, Bz, o, Bo, st6, mv, Bst, lg, Blg, lb, Blb)
            fw.dma(fw.sp, xdst[tsl, :], o[:], reads=[Bo], writes=[C.BxsB])
            if i % 16 == 15 and i != NT - 1:
                fw.barrier()
        fw.barrier()


def phase5a(C, l, xsrc):
    fw, nc = C.fw, C.nc
    S = C.S
    TB = 512
    NB = S // TB
    NCH = 2 * DFF // 128
    with ExitStack() as es:
        WU = alloc(es, nc, "p5wu", [128, 8, 2 * DFF], BF16)
        wu, Bwu = WU
        load_weight(C, WU, C.w_up[l], 8)
        cw, Bcw = alloc(es, nc, "p5cw", [128, NCH, 3], F32)
        cb, Bcb = alloc(es, nc, "p5cb", [128, NCH], F32)
        for k in range(3):
            fw.dma(fw.sp, cw[:, :, k], C.conv_w[l][k].rearrange("(c p) -> p c", p=128), writes=[Bcw], allow_slow_non_contiguous=True)
        fw.dma(fw.sp, cb[:], C.conv_b[l].rearrange("(c p) -> p c", p=128), writes=[Bcb], allow_slow_non_contiguous=True)
        carry, Bcarry = alloc(es, nc, "p5carry", [128, NCH, 2], F32)
        fw.op(fw.pool, lambda e: e.memset(carry[:], 0.0), writes=[Bcarry])
        XT = alloc(es, nc, "p5xt", [128, 1024], F32, n=2)
        xT, BxT = alloc(es, nc, "p5xT", [128, 8, TB], BF16)
        UB = alloc(es, nc, "p5ub", [128, TB + 2], F32, n=2)
        CA = alloc(es, nc, "p5ca", [128, TB], F32, n=2)
        CBb = alloc(es, nc, "p5cbb", [128, TB], F32, n=2)
        T1 = alloc(es, nc, "p5t1", [128, TB], F32, n=2)
        GO = alloc(es, nc, "p5go", [128, TB], BF16, n=2)
        P01, BP01 = alloc(es, nc, "p5P01", [128, 1024], F32, psum=True)
        PU = alloc(es, nc, "p5pu", [128, 512], F32, n=4, psum=True)
        cu = 0
        c_gelu = 2.0 * math.sqrt(2.0 / math.pi)
        for b in range(NB):
            for q in range(TB // 128):
                xt, Bxt = XT[q % 2]
                i = b * (TB // 128) + q
                fw.dma(fw.sp, xt[:], xsrc[i * 128:(i + 1) * 128, :], writes=[Bxt])
                for c in range(8):
                    fw.op(fw.pe, lambda e: e.transpose(P01[:, c * 128:(c + 1) * 128], xt[:, c * 128:(c + 1) * 128], C.identf[:]),
                          reads=[Bxt, C.Bidentf], writes=[BP01])
                fw.op(fw.act, lambda e: e.copy(xT[:, :, q * 128:(q + 1) * 128], P01[:].rearrange("p (c t) -> p c t", t=128)),
                      reads=[BP01], writes=[BxT])
            for fc in range(NCH // 2):
                outs = []
                for half, (CC, chunk) in enumerate(((CA, fc), (CBb, fc + NCH // 2))):
                    pu, Bpu = PU[cu % 4]
                    ub, Bub = UB[cu % 2]
                    cc, Bcc = CC[fc % 2]
                    cu += 1
                    for c in range(8):
                        fw.op(fw.pe, lambda e: e.matmul(pu[:], wu[:, c, chunk * 128:(chunk + 1) * 128], xT[:, c, :],
                                                        start=(c == 0), stop=(c == 7)), reads=[Bwu, BxT], writes=[Bpu])
                    fw.op(fw.act, lambda e: e.copy(ub[:, 2:TB + 2], pu[:]), reads=[Bpu], writes=[Bub])
                    fw.op(fw.pool, lambda e: e.tensor_copy(out=ub[:, 0:2], in_=carry[:, chunk, :]), reads=[Bcarry], writes=[Bub])
                    fw.op(fw.pool, lambda e: e.tensor_copy(out=carry[:, chunk, :], in_=ub[:, TB:TB + 2]), reads=[Bub], writes=[Bcarry])
                    fw.op(fw.act, lambda e: e.activation(out=cc[:], in_=ub[:, 2:TB + 2], func=AF.Identity, scale=cw[:, chunk, 2:3],
                                                         bias=cb[:, chunk:chunk + 1]), reads=[Bub, Bcw, Bcb], writes=[Bcc])
                    fw.op(fw.dve, lambda e: e.scalar_tensor_tensor(out=cc[:], in0=ub[:, 1:TB + 1], scalar=cw[:, chunk, 1:2], in1=cc[:],
                                                                   op0=ALU.mult, op1=ALU.add), reads=[Bub, Bcw, Bcc], writes=[Bcc])
                    fw.op(fw.dve, lambda e: e.scalar_tensor_tensor(out=cc[:], in0=ub[:, 0:TB], scalar=cw[:, chunk, 0:1], in1=cc[:],
                                                                   op0=ALU.mult, op1=ALU.add), reads=[Bub, Bcw, Bcc], writes=[Bcc])
                    outs.append((cc, Bcc))
                (a, Ba), (bb, Bbb) = outs
                go, Bgo = GO[fc % 2]
                t1, Bt1 = T1[fc % 2]
                fw.op(fw.pool, lambda e: e.tensor_tensor(out=t1[:], in0=a[:], in1=a[:], op=ALU.mult), reads=[Ba], writes=[Bt1])
                fw.op(fw.dve, lambda e: e.tensor_scalar(out=t1[:], in0=t1[:], scalar1=0.044715, scalar2=1.0, op0=ALU.mult, op1=ALU.add),
                      reads=[Bt1], writes=[Bt1])
                fw.op(fw.dve, lambda e: e.tensor_tensor(out=t1[:], in0=t1[:], in1=a[:], op=ALU.mult), reads=[Bt1, Ba], writes=[Bt1])
                fw.op(fw.act, lambda e: e.activation(out=t1[:], in_=t1[:], func=AF.Sigmoid, scale=c_gelu), reads=[Bt1], writes=[Bt1])
                fw.op(fw.dve, lambda e: e.tensor_tensor(out=t1[:], in0=t1[:], in1=a[:], op=ALU.mult), reads=[Bt1, Ba], writes=[Bt1])
                fw.op(fw.pool, lambda e: e.tensor_tensor(out=go[:], in0=t1[:], in1=bb[:], op=ALU.mult), reads=[Bt1, Bbb], writes=[Bgo])
                fw.dma(fw.sp, C.gT[b * 4:(b + 1) * 4, :, fc, :].rearrange("q p t -> p q t"), go[:].rearrange("p (q t) -> p q t", t=128),
                       reads=[Bgo], writes=[C.BgT])
            if b % 4 == 3 and b != NB - 1:
                fw.barrier()
        fw.barrier()


def phase5b(C, l, xsrc, xdst, Bdst):
    fw, nc = C.fw, C.nc
    NT = C.NT
    NF = DFF // 128
    with ExitStack() as es:
        WD = alloc(es, nc, "p6wd", [128, NF, 1024], BF16)
        wd, Bwd = WD
        load_weight(C, WD, C.w_down[l], NF)
        lg, Blg = bcast_load(C, es, "p6lg", C.ln2_g[l])
        lb, Blb = bcast_load(C, es, "p6lb", C.ln2_b[l])
        XT = alloc(es, nc, "p6xt", [128, 1024], F32, n=2)
        GT = alloc(es, nc, "p6gt", [128, NF, 128], BF16, n=2)
        ZZ = alloc(es, nc, "p6z", [128, 1024], F32, n=2)
        OUT = alloc(es, nc, "p6out", [128, 1024], F32, n=2)
        ST6 = alloc(es, nc, "p6st6", [128, 2, 6], F32, n=2)
        MV = [es.enter_context(nc.sbuf_tensor(uname("p6mv"), [128, 4], F32)) for _ in range(2)]
        PY = alloc(es, nc, "p6py", [128, 1024], F32, n=2, psum=True)
        for i in range(NT):
            s = i % 2
            xt, Bxt = XT[s]
            gt, Bgt = GT[s]
            py, Bpy = PY[s]
            z, Bz = ZZ[s]
            st6, Bst = ST6[s]
            mv = MV[s]
            tsl = slice(i * 128, (i + 1) * 128)
            fw.dma(fw.sp, xt[:], xsrc[tsl, :], writes=[Bxt])
            fw.dma(fw.sp, gt[:], C.gT[i], reads=[C.BgT], writes=[Bgt])
            for hb in range(2):
                for c in range(NF):
                    fw.op(fw.pe, lambda e: e.matmul(py[:, hb * 512:(hb + 1) * 512], gt[:, c, :], wd[:, c, hb * 512:(hb + 1) * 512],
                                                    start=(c == 0), stop=(c == NF - 1)), reads=[Bgt, Bwd], writes=[Bpy])
            fw.op(fw.dve, lambda e: e.scalar_tensor_tensor(out=z[:], in0=xt[:], scalar=DN_ALPHA, in1=py[:], op0=ALU.mult, op1=ALU.add),
                  reads=[Bxt, Bpy], writes=[Bz])
            o, Bo = OUT[s]
            layer_norm_tile(C, z, Bz, o, Bo, st6, mv, Bst, lg, Blg, lb, Blb)
            fw.dma(fw.sp, xdst[tsl, :], o[:], reads=[Bo], writes=[Bdst])
            if i % 16 == 15 and i != NT - 1:
                fw.barrier()
        fw.barrier()


def build(S=8192, depth=DEPTH, stop_after=None, dbg=False, only=None):
    nc = bass.Bass("TRN2", target_bir_lowering=False)
    C = Ctx()
    C.nc = nc
    C.S = S
    C.NT = S // 128
    NT = C.NT

    def din(name, shape, dt=F32):
        return nc.dram_tensor(name, shape, dt, kind="ExternalInput").ap()

    def dscr(name, shape, dt):
        return nc.dram_tensor(name, shape, dt, kind="Internal").ap()
    C.x = din("x", [S, D])
    C.pos = din("pos", [128, NT], I32)
    C.w_in = din("w_in", [DEPTH, D, INW])
    C.gla_w_gate = din("gla_w_gate", [DEPTH, 16, 512])
    C.gla_b_gate = din("gla_b_gate", [DEPTH, 512])
    C.gla_norm_g = din("gla_norm_g", [DEPTH, 1024])
    C.p_attn = din("p_attn", [DEPTH, 512, D])
    C.p_gla = din("p_gla", [DEPTH, 1024, D])
    C.w_mix_out = din("w_mix_out", [DEPTH, D, D])
    C.b_mix_out = din("b_mix_out", [DEPTH, D])
    C.ln1_g = din("ln1_g", [DEPTH, D])
    C.ln1_b = din("ln1_b", [DEPTH, D])
    C.w_up = din("w_up", [DEPTH, D, 2 * DFF])
    C.conv_w = din("conv_w", [DEPTH, 3, 2 * DFF])
    C.conv_b = din("conv_b", [DEPTH, 2 * DFF])
    C.w_down = din("w_down", [DEPTH, DFF, D])
    C.ln2_g = din("ln2_g", [DEPTH, D])
    C.ln2_b = din("ln2_b", [DEPTH, D])
    C.out = nc.dram_tensor("out", [S, D], F32, kind="ExternalOutput").ap()
    if dbg:
        C.dbg_ao = nc.dram_tensor("dbg_ao", [512, S], BF16, kind="ExternalOutput").ap()
    C.xsA = dscr("xsA", [S, D], F32)
    C.xsB = dscr("xsB", [S, D], F32)
    C.qiT = dscr("qiT", [1024, S], BF16)
    C.kT = dscr("kT", [512, S], BF16)
    C.ikT = dscr("ikT", [64, S], BF16)
    C.va = dscr("va", [S, 520], BF16)
    C.iw = dscr("iw", [S, 8], F32)
    C.aoT = C.dbg_ao if dbg else dscr("aoT", [512, S], BF16)
    C.gT = dscr("gT", [NT, 128, DFF // 128, 128], BF16)
    C.oT = dscr("oT", [1024, S], BF16)
    for n in ["xsA", "xsB", "qiT", "kT", "ikT", "va", "iw", "aoT", "gT", "out", "oT"]:
        setattr(C, "B" + n, Buf(n))
    with ExitStack() as es:
        C.es = es
        C.fw = FW(nc, es)
        es.enter_context(nc.Block())
        setup_consts(C)
        for l in range(depth):
            xsrc = C.x if l == 0 else C.xsA
            last = (l == depth - 1)
            want = lambda p: (only is None or p in only)
            if want("1"):
                phase1(C, l, xsrc)
            if want("2"):
                phase2(C, l)
            if stop_after == "p2":
                break
            if want("3"):
                phase3(C, l, xsrc)
            if want("4"):
                phase4(C, l, xsrc, C.out if stop_after == "p4" else C.xsB)
            if stop_after == "p4":
                break
            if want("a"):
                phase5a(C, l, C.xsB)
            if stop_after == "p5a":
                break
            if want("b"):
                phase5b(C, l, C.xsB, C.out if last else C.xsA, C.Bout if last else C.BxsA)
        C.fw.barrier()
        print("ninstr", C.fw.ninstr)
    return nc


_WNAMES = ["w_in", "gla_w_gate", "gla_b_gate", "gla_norm_g", "p_attn", "p_gla", "w_mix_out", "b_mix_out",
           "ln1_g", "ln1_b", "w_up", "conv_w", "conv_b", "w_down", "ln2_g", "ln2_b"]


def kernel(**inputs):
    x = np.asarray(inputs["x"])
    pos = np.asarray(inputs["positions"])
    B, S, _ = x.shape
    nc = build(S=S, depth=DEPTH)
    wts = {k: np.ascontiguousarray(np.asarray(inputs[k], dtype=np.float32)) for k in _WNAMES}
    in_maps = []
    for b in range(B):
        m = dict(wts)
        m["x"] = np.ascontiguousarray(x[b], dtype=np.float32)
        m["pos"] = np.ascontiguousarray(pos[b].astype(np.int32).reshape(S // 128, 128).T)
        in_maps.append(m)
    res = run_bass_kernel_spmd(nc, in_maps, core_ids=list(range(B)))
    return np.stack([np.asarray(r["out"]) for r in res.results], axis=0).astype(np.float32)
```

```python
import math
import numpy as np
import concourse.bass as bass
import concourse.mybir as mybir
from concourse.bass_utils import run_bass_kernel_spmd
from contextlib import ExitStack

F32 = mybir.dt.float32
BF16 = mybir.dt.bfloat16
I32 = mybir.dt.int32
AF = mybir.ActivationFunctionType
ALU = mybir.AluOpType
AX = mybir.AxisListType

D = 1024
DEPTH = 4
INW = 7256
DFF = 2816
DN_ALPHA = (2.0 * DEPTH) ** 0.25
LN_EPS = 1e-5
TOPK = 256
NBIS = 12
import os as _os
SAME_SYNC = _os.environ.get("SAME_SYNC", "1") == "1"
NEG = -1.0e30


_ALL_BUFS = []


class Buf:
    __slots__ = ("name", "w", "r")

    def __init__(self, name):
        self.name = name
        self.w = None
        self.r = {}
        _ALL_BUFS.append(self)


class Eng:
    def __init__(self, fw, key, eng, sem):
        self.fw = fw
        self.key = key
        self.eng = eng
        self.sem = sem
        self.n = 0
        self.waited = {}

    def wait(self, tok):
        if tok is None:
            return
        key, val = tok
        if self.waited.get(key, 0) >= val:
            return
        self.waited[key] = val
        self.eng.wait_ge(self.fw.semof[key], val)


class FW:
    def __init__(self, nc, es, n_dma_slots=8, same_engine_sync=SAME_SYNC):
        self.nc = nc
        self.es = es
        self.semof = {}
        self.same = same_engine_sync
        self.engs = {}
        for key, eng in (("pe", nc.tensor), ("act", nc.scalar), ("dve", nc.vector),
                         ("pool", nc.gpsimd), ("sp", nc.sync)):
            sem = es.enter_context(nc.semaphore("sem_" + key))
            self.semof[key] = sem
            self.engs[key] = Eng(self, key, eng, sem)
        self.pe, self.act, self.dve, self.pool, self.sp = (self.engs[k] for k in ("pe", "act", "dve", "pool", "sp"))
        self.slots = {}
        for q in ("sp", "pool"):
            lst = []
            for i in range(n_dma_slots):
                key = f"dma_{q}{i}"
                sem = es.enter_context(nc.semaphore(key))
                self.semof[key] = sem
                lst.append([key, 0])
            self.slots[q] = [lst, 0]
        self.ninstr = 0
        self.bsem = es.enter_context(nc.semaphore("bar_arrive"))
        self.gsem = es.enter_context(nc.semaphore("bar_go"))
        self.nbar = 0

    @staticmethod
    def _deps(reads, writes):
        deps = {}

        def add(tok):
            if tok is None:
                return
            k, v = tok
            if deps.get(k, 0) < v:
                deps[k] = v
        for b in reads:
            add(b.w)
        for b in writes:
            add(b.w)
            for k, v in b.r.items():
                add((k, v))
        return deps

    def op(self, E, fn, reads=(), writes=()):
        deps = self._deps(reads, writes)
        for k, v in deps.items():
            if k == E.key and (E.key == "pe" or not self.same):
                continue
            E.wait((k, v))
        ins = fn(E.eng)
        E.n += 1
        ins.then_inc(E.sem, 1)
        self.ninstr += 1
        tok = (E.key, E.n)
        for b in reads:
            if b.r.get(E.key, 0) < E.n:
                b.r[E.key] = E.n
        for b in writes:
            b.w = tok
            b.r = {}
        return tok

    def dma(self, Q, out, in_, reads=(), writes=(), **kw):
        lst, rr = self.slots[Q.key]
        slot = lst[rr % len(lst)]
        self.slots[Q.key][1] = rr + 1
        key, val = slot
        if val > 0:
            Q.wait((key, val))
        deps = self._deps(reads, writes)
        for k, v in deps.items():
            Q.wait((k, v))
        Q.eng.dma_start(out=out, in_=in_, **kw).then_inc(self.semof[key], 16)
        self.ninstr += 1
        slot[1] = val + 16
        tok = (key, val + 16)
        for b in reads:
            if b.r.get(key, 0) < val + 16:
                b.r[key] = val + 16
        for b in writes:
            b.w = tok
            b.r = {}
        return tok

    def barrier(self):
        toks = []
        for k, E in self.engs.items():
            if E.n > 0:
                toks.append((k, E.n))
        for q, (lst, rr) in self.slots.items():
            for key, val in lst:
                if val > 0:
                    toks.append((key, val))
        for k, E in self.engs.items():
            for tok in toks:
                if tok[0] == k:
                    continue
                E.wait(tok)
        for k, E in self.engs.items():
            if E.n > 20000:
                sem = self.es.enter_context(self.nc.semaphore(uname("sem_" + k)))
                self.semof[k] = sem
                E.sem = sem
                E.n = 0
                for E2 in self.engs.values():
                    E2.waited.pop(k, None)
        for b in _ALL_BUFS:
            b.w = None
            b.r = {}


class Ctx:
    pass


_UID = [0]


def uname(name):
    _UID[0] += 1
    return f"{name}_u{_UID[0]}"


def alloc(es, nc, name, shape, dt, n=1, psum=False):
    out = []
    for i in range(n):
        nm = uname(f"{name}{i}" if n > 1 else name)
        if psum:
            t = es.enter_context(nc.psum_tensor(nm, shape, dt))
        else:
            t = es.enter_context(nc.sbuf_tensor(nm, shape, dt))
        out.append((t, Buf(nm)))
    return out if n > 1 else out[0]


def setup_consts(C):
    fw, nc, es = C.fw, C.nc, C.es
    NT = C.NT
    C.identf, C.Bidentf = alloc(es, nc, "identf", [128, 128], F32)
    C.identb, C.Bidentb = alloc(es, nc, "identb", [128, 128], BF16)
    C.onesf, C.Bonesf = alloc(es, nc, "onesf", [128, 128], F32)
    C.cs, C.Bcs = alloc(es, nc, "cs", [128, NT, 16], F32)
    C.epsb, C.Bepsb = alloc(es, nc, "epsb", [128, 1], F32)
    C.oneb = es.enter_context(nc.sbuf_tensor("oneb", [128, 1], F32))
    fw.op(fw.pool, lambda e: e.memset(C.epsb[:], LN_EPS), writes=[C.Bepsb])
    fw.op(fw.pool, lambda e: e.memset(C.oneb[:], 1.0), writes=[C.Bepsb])
    identf, Bi = C.identf, C.Bidentf
    fw.op(fw.pool, lambda e: e.memset(identf[:], 0.0), writes=[Bi])
    fw.op(fw.pool, lambda e: e.affine_select(out=identf[:], in_=identf[:], pattern=[[-1, 128]], compare_op=ALU.not_equal,
                                             fill=1.0, base=0, channel_multiplier=1), reads=[Bi], writes=[Bi])
    fw.op(fw.pool, lambda e: e.tensor_copy(out=C.identb[:], in_=identf[:]), reads=[Bi], writes=[C.Bidentb])
    fw.op(fw.pool, lambda e: e.memset(C.onesf[:], 1.0), writes=[C.Bonesf])
    with ExitStack() as es2:
        posi, Bposi = alloc(es2, nc, "posi", [128, NT], I32)
        posf, Bposf = alloc(es2, nc, "posf", [128, NT], F32)
        R, BR = alloc(es2, nc, "ropeR", [128, NT, 16], F32)
        Ri, BRi = alloc(es2, nc, "ropeRi", [128, NT, 16], I32)
        Rk, BRk = alloc(es2, nc, "ropeRk", [128, NT, 16], F32)
        fw.dma(fw.sp, posi[:], C.pos[:, :], writes=[Bposi])
        fw.op(fw.dve, lambda e: e.tensor_copy(out=posf[:], in_=posi[:]), reads=[Bposi], writes=[Bposf])
        for j in range(8):
            cj = (500000.0 ** (-(2.0 * j) / 16.0)) / (2.0 * math.pi)
            fw.op(fw.dve, lambda e: e.tensor_scalar(out=R[:, :, j], in0=posf[:], scalar1=cj, scalar2=0.25, op0=ALU.mult, op1=ALU.add),
                  reads=[Bposf], writes=[BR])
            fw.op(fw.dve, lambda e: e.tensor_scalar(out=R[:, :, 8 + j], in0=posf[:], scalar1=cj, scalar2=None, op0=ALU.mult),
                  reads=[Bposf], writes=[BR])
        fw.op(fw.dve, lambda e: e.tensor_copy(out=Ri[:], in_=R[:]), reads=[BR], writes=[BRi])
        fw.op(fw.dve, lambda e: e.tensor_copy(out=Rk[:], in_=Ri[:]), reads=[BRi], writes=[BRk])
        fw.op(fw.dve, lambda e: e.tensor_tensor(out=R[:], in0=R[:], in1=Rk[:], op=ALU.subtract), reads=[BR, BRk], writes=[BR])
        fw.op(fw.dve, lambda e: e.tensor_scalar(out=Rk[:], in0=R[:], scalar1=0.5, scalar2=None, op0=ALU.is_gt), reads=[BR], writes=[BRk])
        fw.op(fw.dve, lambda e: e.tensor_tensor(out=R[:], in0=R[:], in1=Rk[:], op=ALU.subtract), reads=[BR, BRk], writes=[BR])
        fw.op(fw.dve, lambda e: e.tensor_scalar(out=Rk[:], in0=R[:], scalar1=-0.5, scalar2=None, op0=ALU.is_lt), reads=[BR], writes=[BRk])
        fw.op(fw.dve, lambda e: e.tensor_tensor(out=R[:], in0=R[:], in1=Rk[:], op=ALU.add), reads=[BR, BRk], writes=[BR])
        fw.op(fw.act, lambda e: e.activation(out=C.cs[:], in_=R[:], func=AF.Sin, scale=2.0 * math.pi * (1.0 - 1e-6)),
              reads=[BR], writes=[C.Bcs])
        fw.barrier()


def load_weight(C, dst, src, nchunk):
    t, B = dst
    for c in range(nchunk):
        C.fw.dma(C.fw.pool, t[:, c, :], src[c * 128:(c + 1) * 128, :], writes=[B])


def phase1(C, l, xsrc):
    fw, nc = C.fw, C.nc
    NT = C.NT
    with ExitStack() as es:
        W1 = alloc(es, nc, "w1", [128, 8, 2120], BF16)
        w1, Bw1 = W1
        load_weight(C, W1, C.w_in[l][:, 0:2120], 8)
        XT = alloc(es, nc, "p1xt", [128, 1024], F32, n=2)
        XTT = alloc(es, nc, "p1xT", [128, 8, 128], BF16, n=2)
        pr, Bpr = alloc(es, nc, "p1pr", [128, 25, 64], F32)
        tmp, Btmp = alloc(es, nc, "p1tmp", [128, 4, 25, 8], F32)
        prb, Bprb = alloc(es, nc, "p1prb", [128, 1600], BF16)
        VA = alloc(es, nc, "p1va", [128, 8, 65], BF16, n=2)
        WS = alloc(es, nc, "p1ws", [128, 8], F32, n=2)
        TT = alloc(es, nc, "p1tT", [128, 13, 128], BF16, n=2)
        pT, BpT = alloc(es, nc, "p1pT", [128, 8, 128], F32, psum=True)
        PG = alloc(es, nc, "p1pg", [128, 512], F32, n=4, psum=True)
        ptb, Bptb = alloc(es, nc, "p1ptb", [128, 16, 128], BF16, psum=True)
        for s in range(2):
            fw.op(fw.pool, lambda e: e.memset(VA[s][0][:], 1.0), writes=[VA[s][1]])
        groups = [(0, 512), (1536, 512), (512, 512), (1024, 512), (2048, 72)]
        wscale = (64.0 ** -0.5) * (8.0 ** -0.5)
        for i in range(NT):
            s = i % 2
            xt, Bxt = XT[s]
            xT, BxT = XTT[s]
            va, Bva = VA[s]
            ws, Bws = WS[s]
            tT, BtT = TT[s]
            fw.dma(fw.sp, xt[:], xsrc[i * 128:(i + 1) * 128, :], writes=[Bxt])
            for c in range(8):
                fw.op(fw.pe, lambda e: e.transpose(pT[:, c, :], xt[:, c * 128:(c + 1) * 128], C.identf[:]),
                      reads=[Bxt, C.Bidentf], writes=[BpT])
            fw.op(fw.act, lambda e: e.copy(xT[:], pT[:]), reads=[BpT], writes=[BxT])
            for gi, (c0, w) in enumerate(groups):
                pg, Bpg = PG[gi % 4]
                for c in range(8):
                    fw.op(fw.pe, lambda e: e.matmul(pg[:, 0:w], xT[:, c, :], w1[:, c, c0:c0 + w], start=(c == 0), stop=(c == 7)),
                          reads=[BxT, Bw1], writes=[Bpg])
                if gi < 3:
                    fw.op(fw.act, lambda e: e.copy(pr[:, gi * 8:(gi + 1) * 8, :], pg[:].rearrange("p (h d) -> p h d", d=64)),
                          reads=[Bpg], writes=[Bpr])
                elif gi == 3:
                    fw.op(fw.act, lambda e: e.copy(va[:, :, 0:64], pg[:].rearrange("p (h d) -> p h d", d=64)),
                          reads=[Bpg], writes=[Bva])
                else:
                    fw.op(fw.act, lambda e: e.copy(pr[:, 24, :], pg[:, 0:64]), reads=[Bpg], writes=[Bpr])
                    fw.op(fw.act, lambda e: e.mul(ws[:], pg[:, 64:72], wscale), reads=[Bpg], writes=[Bws])
            cos = C.cs[:, i:i + 1, 0:8].to_broadcast([128, 25, 8])
            sin = C.cs[:, i:i + 1, 8:16].to_broadcast([128, 25, 8])
            x1 = pr[:, :, 0:8]
            x2 = pr[:, :, 8:16]
            fw.op(fw.dve, lambda e: e.tensor_tensor(out=tmp[:, 0], in0=x1, in1=cos, op=ALU.mult), reads=[Bpr, C.Bcs], writes=[Btmp])
            fw.op(fw.dve, lambda e: e.tensor_tensor(out=tmp[:, 1], in0=x2, in1=sin, op=ALU.mult), reads=[Bpr, C.Bcs], writes=[Btmp])
            fw.op(fw.dve, lambda e: e.tensor_tensor(out=tmp[:, 2], in0=x2, in1=cos, op=ALU.mult), reads=[Bpr, C.Bcs], writes=[Btmp])
            fw.op(fw.dve, lambda e: e.tensor_tensor(out=tmp[:, 3], in0=x1, in1=sin, op=ALU.mult), reads=[Bpr, C.Bcs], writes=[Btmp])
            fw.op(fw.dve, lambda e: e.tensor_tensor(out=x1, in0=tmp[:, 0], in1=tmp[:, 1], op=ALU.subtract), reads=[Btmp], writes=[Bpr])
            fw.op(fw.dve, lambda e: e.tensor_tensor(out=x2, in0=tmp[:, 2], in1=tmp[:, 3], op=ALU.add), reads=[Btmp], writes=[Bpr])
            fw.op(fw.pool, lambda e: e.tensor_copy(out=prb[:], in_=pr[:].rearrange("p h d -> p (h d)")), reads=[Bpr], writes=[Bprb])
            for k in range(12):
                fw.op(fw.pe, lambda e: e.transpose(ptb[:, k, :], prb[:, k * 128:(k + 1) * 128], C.identb[:]),
                      reads=[Bprb, C.Bidentb], writes=[Bptb])
            fw.op(fw.pe, lambda e: e.transpose(ptb[0:64, 12, :], prb[:, 1536:1600], C.identb[:]),
                  reads=[Bprb, C.Bidentb], writes=[Bptb])
            fw.op(fw.dve, lambda e: e.tensor_copy(out=tT[:, 0:12, :], in_=ptb[:, 0:12, :]), reads=[Bptb], writes=[BtT])
            fw.op(fw.dve, lambda e: e.tensor_copy(out=tT[0:64, 12, :], in_=ptb[0:64, 12, :]), reads=[Bptb], writes=[BtT])
            tsl = slice(i * 128, (i + 1) * 128)
            fw.dma(fw.sp, C.qiT.rearrange("(c p) t -> p c t", p=128)[:, :, tsl], tT[:, 0:8, :], reads=[BtT], writes=[C.BqiT])
            fw.dma(fw.sp, C.kT.rearrange("(c p) t -> p c t", p=128)[:, :, tsl], tT[:, 8:12, :], reads=[BtT], writes=[C.BkT])
            fw.dma(fw.sp, C.ikT[:, tsl], tT[0:64, 12, :], reads=[BtT], writes=[C.BikT])
            fw.dma(fw.sp, C.va[tsl, :], va[:].rearrange("p h d -> p (h d)"), reads=[Bva], writes=[C.Bva])
            fw.dma(fw.sp, C.iw[tsl, :], ws[:], reads=[Bws], writes=[C.Biw])
        fw.barrier()


def phase2(C, l):
    fw, nc = C.fw, C.nc
    S, NT = C.S, C.NT
    GK = 8
    with ExitStack() as es:
        ik, Bik = alloc(es, nc, "p2ik", [64, S], BF16)
        QI = alloc(es, nc, "p2qi", [64, 16, 128], BF16, n=2)
        WJ = alloc(es, nc, "p2wj", [128, 8], F32, n=2)
        Sc, BSc = alloc(es, nc, "p2S", [128, S], F32)
        junk, Bjunk = alloc(es, nc, "p2junk", [128, S], BF16)
        RR = alloc(es, nc, "p2r", [128, 512], F32, n=2)
        bs, Bbs = alloc(es, nc, "p2bs", [128, 8], F32)
        diag, Bdiag = alloc(es, nc, "p2diag", [128, 128], F32)
        thrb, Bthrb = alloc(es, nc, "p2thrb", [128, 128], F32)
        MASK = alloc(es, nc, "p2mask", [128, NT, 128], BF16, n=2)
        KG = alloc(es, nc, "p2kg", [64, 8, GK * 128], BF16, n=2)
        VG = alloc(es, nc, "p2vg", [128, GK, 520], BF16, n=2)
        PT = alloc(es, nc, "p2pT", [128, 4, 128], BF16, n=3)
        oacc, Boacc = alloc(es, nc, "p2oacc", [128, 8, 65], F32)
        rinv, Brinv = alloc(es, nc, "p2rinv", [128, 8], F32)
        aof, Baof = alloc(es, nc, "p2aof", [128, 8, 64], F32)
        AOT = alloc(es, nc, "p2aoT", [128, 4, 128], BF16, n=2)
        PSS = alloc(es, nc, "p2pss", [128, 512], F32, n=2, psum=True)
        PST = alloc(es, nc, "p2pst", [128, 4, 128], F32, n=2, psum=True)
        PSL = alloc(es, nc, "p2psl", [128, 4, 128], F32, n=2, psum=True)
        PSO = alloc(es, nc, "p2pso", [128, 4, 65], F32, n=2, psum=True)
        fw.dma(fw.sp, ik[:], C.ikT[:, :], reads=[C.BikT], writes=[Bik])
        fw.op(fw.dve, lambda e: e.memset(bs[:], 0.0), writes=[Bbs])
        qiT_v = C.qiT.rearrange("(h d) t -> d h t", d=64)
        kT_v = C.kT.rearrange("(h d) t -> d h t", d=64)
        cn = {"s": 0, "t": 0, "l": 0, "g": 0}
        lo = bs[:, 0:1]
        w0 = bs[:, 1:2]
        mid = bs[:, 2:3]
        cnt = bs[:, 3:4]
        gw = bs[:, 4:5]
        hi = bs[:, 5:6]

        def front(j):
            N = 128 * (j + 1)
            nkt = j + 1
            s = j % 2
            qi, Bqi = QI[s]
            wj, Bwj = WJ[s]
            maskT, BmaskT = MASK[s]
            tsl = slice(j * 128, (j + 1) * 128)
            fw.dma(fw.sp, qi[:], qiT_v[:, :, tsl], reads=[C.BqiT], writes=[Bqi])
            fw.dma(fw.sp, wj[:], C.iw[tsl, :], reads=[C.Biw], writes=[Bwj])
            yield
            for kb in range((N + 511) // 512):
                cols = min(512, N - kb * 512)
                ksl = slice(kb * 512, kb * 512 + cols)
                for h in range(8):
                    pss, Bpss = PSS[cn["s"] % 2]
                    r, Br = RR[cn["s"] % 2]
                    cn["s"] += 1
                    fw.op(fw.pe, lambda e: e.matmul(pss[:, 0:cols], qi[:, 8 + h, :], ik[:, ksl], start=True, stop=True),
                          reads=[Bqi, Bik], writes=[Bpss])
                    fw.op(fw.act, lambda e: e.activation(out=r[:, 0:cols], in_=pss[:, 0:cols], func=AF.Relu),
                          reads=[Bpss], writes=[Br])
                    if h == 0:
                        fw.op(fw.dve, lambda e: e.tensor_scalar(out=Sc[:, ksl], in0=r[:, 0:cols], scalar1=wj[:, 0:1], scalar2=None,
                                                                op0=ALU.mult), reads=[Br, Bwj], writes=[BSc])
                    else:
                        fw.op(fw.dve, lambda e: e.scalar_tensor_tensor(out=Sc[:, ksl], in0=r[:, 0:cols], scalar=wj[:, h:h + 1],
                                                                       in1=Sc[:, ksl], op0=ALU.mult, op1=ALU.add),
                              reads=[Br, Bwj, BSc], writes=[BSc])
                    yield
            fw.op(fw.dve, lambda e: e.memset(Sc[0:64, N - 64:N], NEG), writes=[BSc])
            if j < 2:
                fw.op(fw.dve, lambda e: e.memset(lo, -1.0e29), writes=[Bbs])
            else:
                fw.op(fw.dve, lambda e: e.tensor_reduce(out=hi, in_=Sc[:, 0:N], axis=AX.X, op=ALU.max), reads=[BSc], writes=[Bbs])
                fw.op(fw.dve, lambda e: e.tensor_reduce(out=lo, in_=Sc[:, 0:N - 64], axis=AX.X, op=ALU.min), reads=[BSc], writes=[Bbs])
                fw.op(fw.dve, lambda e: e.tensor_tensor(out=w0, in0=hi, in1=lo, op=ALU.subtract), reads=[Bbs], writes=[Bbs])
                yield
                for it in range(1, NBIS + 1):
                    sc = 2.0 ** (-it)
                    fw.op(fw.dve, lambda e: e.scalar_tensor_tensor(out=mid, in0=w0, scalar=sc, in1=lo, op0=ALU.mult, op1=ALU.add),
                          reads=[Bbs], writes=[Bbs])

                    def f2(e):
                        e.tensor_scalar(out=junk[:, 0:N], in0=Sc[:, 0:N], scalar1=mid, scalar2=0.0, op0=ALU.is_ge, op1=ALU.add,
                                        accum_out=cnt)
                        return e.tensor_copy(out=bs[:, 6:7], in_=bs[:, 7:8])
                    fw.op(fw.dve, f2, reads=[Bbs, BSc], writes=[Bbs, Bjunk])
                    fw.op(fw.dve, lambda e: e.scalar_tensor_tensor(out=gw, in0=cnt, scalar=TOPK - 0.5, in1=w0, op0=ALU.is_ge, op1=ALU.mult),
                          reads=[Bbs], writes=[Bbs])
                    fw.op(fw.dve, lambda e: e.scalar_tensor_tensor(out=lo, in0=gw, scalar=sc, in1=lo, op0=ALU.mult, op1=ALU.add),
                          reads=[Bbs], writes=[Bbs])
                    for _ in range(max(1, N // 512)):
                        yield
            fw.op(fw.dve, lambda e: e.tensor_scalar(out=diag[:], in0=C.identf[:], scalar1=lo, scalar2=None, op0=ALU.mult),
                  reads=[Bbs, C.Bidentf], writes=[Bdiag])
            pst, Bpst = PST[cn["t"] % 2]
            cn["t"] += 1
            fw.op(fw.pe, lambda e: e.matmul(pst[:, 0, :], C.onesf[:], diag[:], start=True, stop=True),
                  reads=[Bdiag, C.Bonesf], writes=[Bpst])
            fw.op(fw.dve, lambda e: e.tensor_copy(out=thrb[:], in_=pst[:, 0, :]), reads=[Bpst], writes=[Bthrb])
            yield
            for k0 in range(0, nkt, 4):
                n4 = min(4, nkt - k0)
                pst, Bpst = PST[cn["t"] % 2]
                cn["t"] += 1
                for a in range(n4):
                    kt = k0 + a
                    fw.op(fw.pe, lambda e: e.transpose(pst[:, a, :], Sc[:, kt * 128:(kt + 1) * 128], C.identf[:]),
                          reads=[BSc, C.Bidentf], writes=[Bpst])
                fw.op(fw.dve, lambda e: e.tensor_tensor(out=maskT[:, k0:k0 + n4, :], in0=pst[:, 0:n4, :],
                                                        in1=thrb[:, None, :].to_broadcast([128, n4, 128]), op=ALU.is_ge),
                      reads=[Bpst, Bthrb], writes=[BmaskT])
                yield

        def attn(j):
            nkt = j + 1
            s = j % 2
            qi, Bqi = QI[s]
            maskT, BmaskT = MASK[s]
            tsl = slice(j * 128, (j + 1) * 128)
            ng = (nkt + GK - 1) // GK
            for g in range(ng):
                nk = min(GK, nkt - g * GK)
                kg, Bkg = KG[cn["g"] % 2]
                vg, Bvg = VG[cn["g"] % 2]
                cn["g"] += 1
                k0g = g * GK
                fw.dma(fw.sp, kg[:, :, 0:nk * 128], kT_v[:, :, k0g * 128:(k0g + nk) * 128], reads=[C.BkT], writes=[Bkg])
                fw.dma(fw.sp, vg[:, 0:nk, :], C.va[k0g * 128:(k0g + nk) * 128, :].rearrange("(k p) c -> p k c", p=128),
                       reads=[C.Bva], writes=[Bvg])
                yield
                steps = [(h, q4) for h in range(8) for q4 in range(0, nk, 4)]
                pend = None
                for (h, q4) in steps:
                    n4 = min(4, nk - q4)
                    psl, Bpsl = PSL[cn["l"] % 2]
                    pT, BpT = PT[cn["l"] % 3]
                    cn["l"] += 1
                    for a in range(n4):
                        fw.op(fw.pe, lambda e: e.matmul(psl[:, a, :], kg[:, h, (q4 + a) * 128:(q4 + a + 1) * 128], qi[:, h, :],
                                                        start=True, stop=True), reads=[Bkg, Bqi], writes=[Bpsl])
                    fw.op(fw.act, lambda e: e.activation(out=pT[:, 0:n4, :], in_=psl[:, 0:n4, :], func=AF.Exp, scale=0.125),
                          reads=[Bpsl], writes=[BpT])
                    fw.op(fw.pool, lambda e: e.tensor_tensor(out=pT[:, 0:n4, :], in0=pT[:, 0:n4, :],
                                                             in1=maskT[:, k0g + q4:k0g + q4 + n4, :], op=ALU.mult),
                          reads=[BpT, BmaskT], writes=[BpT])
                    if pend is not None:
                        ph, pq4, pn4, ppT, pBpT = pend
                        ppso, pBpso = PSO[ph // 4]
                        for a in range(pn4):
                            fw.op(fw.pe, lambda e: e.matmul(ppso[:, ph % 4, :], ppT[:, a, :], vg[:, pq4 + a, ph * 65:(ph + 1) * 65],
                                                            start=(pq4 + a == 0), stop=(pq4 + a == nk - 1)),
                                  reads=[pBpT, Bvg], writes=[pBpso])
                    pend = (h, q4, n4, pT, BpT)
                    for _ in range(n4):
                        yield
                ph, pq4, pn4, ppT, pBpT = pend
                ppso, pBpso = PSO[ph // 4]
                for a in range(pn4):
                    fw.op(fw.pe, lambda e: e.matmul(ppso[:, ph % 4, :], ppT[:, a, :], vg[:, pq4 + a, ph * 65:(ph + 1) * 65],
                                                    start=(pq4 + a == 0), stop=(pq4 + a == nk - 1)),
                          reads=[pBpT, Bvg], writes=[pBpso])
                for hb in range(2):
                    pso, Bpso = PSO[hb]
                    if g == 0:
                        fw.op(fw.dve, lambda e: e.tensor_copy(out=oacc[:, hb * 4:(hb + 1) * 4, :], in_=pso[:]), reads=[Bpso], writes=[Boacc])
                    else:
                        fw.op(fw.dve, lambda e: e.tensor_tensor(out=oacc[:, hb * 4:(hb + 1) * 4, :], in0=oacc[:, hb * 4:(hb + 1) * 4, :],
                                                                in1=pso[:], op=ALU.add), reads=[Bpso, Boacc], writes=[Boacc])
            fw.op(fw.dve, lambda e: e.reciprocal(out=rinv[:], in_=oacc[:, :, 64]), reads=[Boacc], writes=[Brinv])
            fw.op(fw.dve, lambda e: e.tensor_tensor(out=aof[:], in0=oacc[:, :, 0:64], in1=rinv[:, :, None].to_broadcast([128, 8, 64]),
                                                    op=ALU.mult), reads=[Boacc, Brinv], writes=[Baof])
            pst, Bpst = PST[cn["t"] % 2]
            cn["t"] += 1
            aoT, BaoT = AOT[s]
            aof2 = aof[:].rearrange("p h d -> p (h d)")
            for c in range(4):
                fw.op(fw.pe, lambda e: e.transpose(pst[:, c, :], aof2[:, c * 128:(c + 1) * 128], C.identf[:]),
                      reads=[Baof, C.Bidentf], writes=[Bpst])
            fw.op(fw.act, lambda e: e.copy(aoT[:], pst[:]), reads=[Bpst], writes=[BaoT])
            fw.dma(fw.sp, C.aoT.rearrange("(c p) t -> p c t", p=128)[:, :, tsl], aoT[:], reads=[BaoT], writes=[C.BaoT])
            yield

        def run_merged(gA, gF):
            la = list_len(gA[1])
            lf = list_len(gF[1]) if gF is not None else 0
            ga = gA[0]
            gf = gF[0] if gF is not None else None
            da = df = 0
            while da < la or df < lf:
                if df < lf and (da >= la or df * la <= da * lf):
                    next(gf, None)
                    df += 1
                else:
                    next(ga, None)
                    da += 1
            for g_ in (ga, gf):
                if g_ is not None:
                    for _ in g_:
                        pass

        def nyield_front(j):
            N = 128 * (j + 1)
            nkt = j + 1
            n = 1 + ((N + 511) // 512) * 8 + 1 + ((NBIS * max(1, N // 512) + 1) if j >= 2 else 0) + (nkt + 3) // 4
            return n

        def nyield_attn(j):
            nkt = j + 1
            return nkt * 8 + (nkt + GK - 1) // GK + 1

        def list_len(n):
            return n

        SEG = 32
        for j0 in range(0, NT, SEG):
            j1 = min(NT, j0 + SEG)
            for _ in front(j0):
                pass
            for j in range(j0, j1):
                gA = (attn(j), nyield_attn(j))
                gF = (front(j + 1), nyield_front(j + 1)) if j + 1 < j1 else None
                run_merged(gA, gF)
            fw.barrier()


def load_xT(C, xt, Bxt, pT, BpT, xT, BxT, src_ap):
    fw = C.fw
    fw.dma(fw.sp, xt[:], src_ap, writes=[Bxt])
    for c in range(8):
        fw.op(fw.pe, lambda e: e.transpose(pT[:, c * 128:(c + 1) * 128], xt[:, c * 128:(c + 1) * 128], C.identf[:]),
              reads=[Bxt, C.Bidentf], writes=[BpT])
    fw.op(fw.act, lambda e: e.copy(xT[:].rearrange("p c t -> p (c t)"), pT[:]), reads=[BpT], writes=[BxT])


def bcast_load(C, es, name, row_ap, n=1024):
    t, B = alloc(es, C.nc, name, [128, n], F32)
    C.fw.dma(C.fw.sp, t[:], row_ap.partition_broadcast(128), writes=[B])
    return t, B


def layer_norm_tile(C, z, Bz, outt, Bout, st6, mv, Bst, g_b, Bg, b_b, Bb):
    fw = C.fw
    for hh in range(2):
        fw.op(fw.dve, lambda e: e.bn_stats(out=st6[:, hh, :], in_=z[:, hh * 512:(hh + 1) * 512]), reads=[Bz], writes=[Bst])
    fw.op(fw.dve, lambda e: e.bn_aggr(out=mv[:, 0:2], in_=st6[:].rearrange("p a b -> p (a b)")), reads=[Bst], writes=[Bst])
    fw.op(fw.act, lambda e: e.activation(out=mv[:, 2:3], in_=mv[:, 1:2], func=AF.Ln, bias=C.epsb[:, 0:1]), reads=[Bst, C.Bepsb], writes=[Bst])
    fw.op(fw.act, lambda e: e.activation(out=mv[:, 3:4], in_=mv[:, 2:3], func=AF.Exp, scale=-0.5), reads=[Bst], writes=[Bst])
    fw.op(fw.dve, lambda e: e.tensor_scalar(out=z[:], in0=z[:], scalar1=mv[:, 0:1], scalar2=mv[:, 3:4], op0=ALU.subtract, op1=ALU.mult),
          reads=[Bz, Bst], writes=[Bz])
    fw.op(fw.dve, lambda e: e.tensor_tensor(out=z[:], in0=z[:], in1=g_b[:], op=ALU.mult), reads=[Bz, Bg], writes=[Bz])
    fw.op(fw.dve, lambda e: e.tensor_tensor(out=outt[:], in0=z[:], in1=b_b[:], op=ALU.add), reads=[Bz, Bb], writes=[Bout])


def phase3(C, l, xsrc):
    fw, nc = C.fw, C.nc
    NT = C.NT
    W0 = 2120
    with ExitStack() as es:
        WG = alloc(es, nc, "p3wg", [128, 8, 3088], BF16)
        wg, Bwg = WG
        load_weight(C, WG, C.w_in[l][:, W0:W0 + 3088], 8)
        wga, Bwga = alloc(es, nc, "p3wga", [33, 512], F32)
        fw.op(fw.pool, lambda e: e.memset(wga[:], 0.0), writes=[Bwga])
        fw.dma(fw.sp, wga[0:16, :], C.gla_w_gate[l], writes=[Bwga])
        fw.dma(fw.sp, wga[32:33, :], C.gla_b_gate[l:l + 1, :], writes=[Bwga])
        gnb, Bgnb = bcast_load(C, es, "p3gnb", C.gla_norm_g[l])
        U, BU = alloc(es, nc, "p3U", [128, 128], F32)
        fw.op(fw.pool, lambda e: e.memset(U[:], -1.0 / 16.0), writes=[BU])
        fw.op(fw.pool, lambda e: e.affine_select(out=U[:], in_=U[:], pattern=[[-1, 128]], compare_op=ALU.is_gt,
                                                 fill=0.0, base=0, channel_multiplier=1), reads=[BU], writes=[BU])
        fw.op(fw.pool, lambda e: e.memset(U[64:128, 0:64], 0.0), reads=[BU], writes=[BU])
        Ind, BInd = alloc(es, nc, "p3Ind", [128, 2], F32)
        fw.op(fw.pool, lambda e: e.memset(Ind[:], 0.0), writes=[BInd])
        fw.op(fw.pool, lambda e: e.memset(Ind[0:64, 0:1], -1.0 / 16.0), reads=[BInd], writes=[BInd])
        fw.op(fw.pool, lambda e: e.memset(Ind[64:128, 1:2], -1.0 / 16.0), reads=[BInd], writes=[BInd])
        glr, Bglr = alloc(es, nc, "p3glr", [33, 128], F32)
        fw.op(fw.pool, lambda e: e.memset(glr[:], 1.0), writes=[Bglr])
        state, Bstate = alloc(es, nc, "p3state", [128, 4, 256], F32)
        fw.op(fw.pool, lambda e: e.memset(state[:], 0.0), writes=[Bstate])
        stb, Bstb = alloc(es, nc, "p3stb", [128, 4, 256], BF16)
        XT = alloc(es, nc, "p3xt", [128, 1024], F32, n=2)
        XTT = alloc(es, nc, "p3xT", [128, 8, 128], BF16, n=2)
        gqT, BgqT = alloc(es, nc, "p3gqT", [128, 4, 128], BF16)
        gvb, Bgvb = alloc(es, nc, "p3gvb", [128, 1024], BF16)
        e1, Be1 = alloc(es, nc, "p3e1", [128, 512], F32)
        spl, Bspl = alloc(es, nc, "p3spl", [128, 512], F32)
        eD, BeD = alloc(es, nc, "p3eD", [128, 512], F32)
        kdec, Bkdec = alloc(es, nc, "p3kdec", [128, 512], BF16)
        etot, Betot = alloc(es, nc, "p3etot", [128, 8], F32)
        st6, Bst = alloc(es, nc, "p3st6", [128, 4, 6], F32)
        mv = es.enter_context(nc.sbuf_tensor(uname("p3mv"), [128, 4, 4], F32))
        on, Bon = alloc(es, nc, "p3on", [128, 1024], F32)
        sil, Bsil = alloc(es, nc, "p3sil", [128, 1024], F32)
        OT = alloc(es, nc, "p3oT", [128, 8, 128], BF16, n=2)
        P01, BP01 = alloc(es, nc, "p3P01", [128, 1024], F32, psum=True)
        P2, BP2 = alloc(es, nc, "p3P2", [128, 512], F32, psum=True)
        P3, BP3 = alloc(es, nc, "p3P3", [128, 512], F32, psum=True)
        P45, BP45 = alloc(es, nc, "p3P45", [128, 1024], F32, psum=True)
        P67, BP67 = alloc(es, nc, "p3P67", [128, 1024], F32, psum=True)
        for i in range(NT):
            s = i % 2
            xt, Bxt = XT[s]
            xT, BxT = XTT[s]
            load_xT(C, xt, Bxt, P01, BP01, xT, BxT, xsrc[i * 128:(i + 1) * 128, :])
            for h in range(4):
                for c in range(8):
                    fw.op(fw.pe, lambda e: e.matmul(P2[:, h * 128:(h + 1) * 128], wg[:, c, h * 128:(h + 1) * 128], xT[:, c, :],
                                                    start=(c == 0), stop=(c == 7)), reads=[Bwg, BxT], writes=[BP2])
            fw.op(fw.act, lambda e: e.mul(gqT[:].rearrange("p h t -> p (h t)"), P2[:], 128.0 ** -0.5), reads=[BP2], writes=[BgqT])
            for c in range(8):
                fw.op(fw.pe, lambda e: e.matmul(P3[:], xT[:, c, :], wg[:, c, 512:1024], start=(c == 0), stop=(c == 7)),
                      reads=[Bwg, BxT], writes=[BP3])
            for hb in range(2):
                for c in range(8):
                    fw.op(fw.pe, lambda e: e.matmul(P45[:, hb * 512:(hb + 1) * 512], xT[:, c, :], wg[:, c, 1024 + hb * 512:1536 + hb * 512],
                                                    start=(c == 0), stop=(c == 7)), reads=[Bwg, BxT], writes=[BP45])
            fw.op(fw.act, lambda e: e.copy(gvb[:], P45[:]), reads=[BP45], writes=[Bgvb])
            for c in range(8):
                fw.op(fw.pe, lambda e: e.matmul(P2[0:16, 0:128], wg[:, c, 2048:2064], xT[:, c, :], start=(c == 0), stop=(c == 7)),
                      reads=[Bwg, BxT], writes=[BP2])
            fw.op(fw.act, lambda e: e.copy(glr[0:16, :], P2[0:16, 0:128]), reads=[BP2], writes=[Bglr])
            fw.op(fw.pe, lambda e: e.matmul(P2[:], glr[:], wga[:], start=True, stop=True), reads=[Bglr, Bwga], writes=[BP2])
            fw.op(fw.act, lambda e: e.activation(out=e1[:], in_=P2[:], func=AF.Exp, scale=-1.0), reads=[BP2], writes=[Be1])
            fw.op(fw.act, lambda e: e.activation(out=spl[:], in_=e1[:], func=AF.Ln, bias=C.oneb[:, 0:1]), reads=[Be1, C.Bepsb], writes=[Bspl])
            fw.op(fw.pe, lambda e: e.matmul(P2[:], U[:], spl[:], start=True, stop=True), reads=[BU, Bspl], writes=[BP2])
            fw.op(fw.act, lambda e: e.activation(out=eD[:], in_=P2[:], func=AF.Exp), reads=[BP2], writes=[BeD])
            fw.op(fw.dve, lambda e: e.tensor_tensor(out=kdec[:], in0=P3[:], in1=eD[:], op=ALU.mult), reads=[BP3, BeD], writes=[Bkdec])
            for h in range(4):
                fw.op(fw.pe, lambda e: e.matmul(P2[:, h * 2:h * 2 + 2], spl[:, h * 128:(h + 1) * 128], Ind[:], start=True, stop=True),
                      reads=[Bspl, BInd], writes=[BP2])
            fw.op(fw.act, lambda e: e.activation(out=etot[:], in_=P2[:, 0:8], func=AF.Exp), reads=[BP2], writes=[Betot])
            for hb in range(2):
                for c in range(8):
                    fw.op(fw.pe, lambda e: e.matmul(P01[:, hb * 512:(hb + 1) * 512], xT[:, c, :], wg[:, c, 2064 + hb * 512:2576 + hb * 512],
                                                    start=(c == 0), stop=(c == 7)), reads=[Bwg, BxT], writes=[BP01])
            for ch in range(2):
                r0 = ch * 64
                for h in range(4):
                    fw.op(fw.pe, lambda e: e.matmul(P45[:, h * 256:(h + 1) * 256], kdec[r0:r0 + 64, h * 128:(h + 1) * 128],
                                                    gvb[r0:r0 + 64, h * 256:(h + 1) * 256], start=True, stop=True),
                          reads=[Bkdec, Bgvb], writes=[BP45])
                for h in range(4):
                    fw.op(fw.dve, lambda e: e.scalar_tensor_tensor(out=state[:, h, :], in0=state[:, h, :], scalar=etot[:, h * 2 + ch:h * 2 + ch + 1],
                                                                   in1=P45[:, h * 256:(h + 1) * 256], op0=ALU.mult, op1=ALU.add),
                          reads=[Bstate, Betot, BP45], writes=[Bstate])
                fw.op(fw.act, lambda e: e.copy(stb[:], state[:]), reads=[Bstate], writes=[Bstb])
                for h in range(4):
                    fw.op(fw.pe, lambda e: e.matmul(P67[r0:r0 + 64, h * 256:(h + 1) * 256], gqT[:, h, r0:r0 + 64], stb[:, h, :],
                                                    start=True, stop=True), reads=[BgqT, Bstb], writes=[BP67])
            for h in range(4):
                fw.op(fw.dve, lambda e: e.bn_stats(out=st6[:, h, :], in_=P67[:, h * 256:(h + 1) * 256]), reads=[BP67], writes=[Bst])
            for h in range(4):
                fw.op(fw.dve, lambda e: e.bn_aggr(out=mv[:, h, 0:2], in_=st6[:, h, :]), reads=[Bst], writes=[Bst])
            fw.op(fw.act, lambda e: e.activation(out=mv[:, :, 2], in_=mv[:, :, 1], func=AF.Ln, bias=C.epsb[:, 0:1]), reads=[Bst, C.Bepsb], writes=[Bst])
            fw.op(fw.act, lambda e: e.activation(out=mv[:, :, 3], in_=mv[:, :, 2], func=AF.Exp, scale=-0.5), reads=[Bst], writes=[Bst])
            for h in range(4):
                fw.op(fw.dve, lambda e: e.tensor_scalar(out=on[:, h * 256:(h + 1) * 256], in0=P67[:, h * 256:(h + 1) * 256],
                                                        scalar1=mv[:, h, 0:1], scalar2=mv[:, h, 3:4], op0=ALU.subtract, op1=ALU.mult),
                      reads=[BP67, Bst], writes=[Bon])
            fw.op(fw.act, lambda e: e.activation(out=sil[:], in_=P01[:], func=AF.Silu), reads=[BP01], writes=[Bsil])
            fw.op(fw.dve, lambda e: e.tensor_tensor(out=on[:], in0=on[:], in1=gnb[:], op=ALU.mult), reads=[Bon, Bgnb], writes=[Bon])
            fw.op(fw.dve, lambda e: e.tensor_tensor(out=on[:], in0=on[:], in1=sil[:], op=ALU.mult), reads=[Bon, Bsil], writes=[Bon])
            oT, BoT = OT[s]
            for c in range(8):
                fw.op(fw.pe, lambda e: e.transpose(P45[:, c * 128:(c + 1) * 128], on[:, c * 128:(c + 1) * 128], C.identf[:]),
                      reads=[Bon, C.Bidentf], writes=[BP45])
            fw.op(fw.act, lambda e: e.copy(oT[:].rearrange("p c t -> p (c t)"), P45[:]), reads=[BP45], writes=[BoT])
            fw.dma(fw.sp, C.oT.rearrange("(c p) t -> p c t", p=128)[:, :, i * 128:(i + 1) * 128], oT[:], reads=[BoT], writes=[C.BoT])
        fw.barrier()


def phase4(C, l, xsrc, xdst):
    fw, nc = C.fw, C.nc
    NT = C.NT
    with ExitStack() as es:
        WGT = alloc(es, nc, "p4wgt", [128, 8, 2048], BF16)
        wgt, Bwgt = WGT
        load_weight(C, WGT, C.w_in[l][:, 5208:7256], 8)
        PA = alloc(es, nc, "p4pa", [128, 4, 1024], BF16)
        load_weight(C, PA, C.p_attn[l], 4)
        PGL = alloc(es, nc, "p4pg", [128, 8, 1024], BF16)
        load_weight(C, PGL, C.p_gla[l], 8)
        WO = alloc(es, nc, "p4wo", [128, 8, 1024], BF16)
        load_weight(C, WO, C.w_mix_out[l], 8)
        pa, Bpa = PA
        pgl, Bpgl = PGL
        wo, Bwo = WO
        bob, Bbob = bcast_load(C, es, "p4bob", C.b_mix_out[l])
        lg, Blg = bcast_load(C, es, "p4lg", C.ln1_g[l])
        lb, Blb = bcast_load(C, es, "p4lb", C.ln1_b[l])
        XT = alloc(es, nc, "p4xt", [128, 1024], F32, n=2)
        XTT = alloc(es, nc, "p4xT", [128, 8, 128], BF16, n=2)
        AO = alloc(es, nc, "p4ao", [128, 4, 128], BF16, n=2)
        OTT = alloc(es, nc, "p4ot", [128, 8, 128], BF16, n=2)
        sa, Bsa = alloc(es, nc, "p4sa", [128, 1024], F32)
        sb_, Bsb = alloc(es, nc, "p4sb", [128, 1024], F32)
        mg, Bmg = alloc(es, nc, "p4mg", [128, 1024], F32)
        mT, BmT = alloc(es, nc, "p4mT", [128, 8, 128], BF16)
        ZZ = alloc(es, nc, "p4z", [128, 1024], F32, n=2)
        OUT = alloc(es, nc, "p4out", [128, 1024], F32, n=2)
        ST6 = alloc(es, nc, "p4st6", [128, 2, 6], F32, n=2)
        MV = [es.enter_context(nc.sbuf_tensor(uname("p4mv"), [128, 4], F32)) for _ in range(2)]
        P01, BP01 = alloc(es, nc, "p4P01", [128, 1024], F32, psum=True)
        P23, BP23 = alloc(es, nc, "p4P23", [128, 1024], F32, psum=True)
        P45, BP45 = alloc(es, nc, "p4P45", [128, 1024], F32, psum=True)
        P67, BP67 = alloc(es, nc, "p4P67", [128, 1024], F32, psum=True)
        for i in range(NT):
            s = i % 2
            xt, Bxt = XT[s]
            xT, BxT = XTT[s]
            ao, Bao = AO[s]
            ot, Bot = OTT[s]
            tsl = slice(i * 128, (i + 1) * 128)
            z, Bz = ZZ[s]
            st6, Bst = ST6[s]
            mv = MV[s]
            load_xT(C, xt, Bxt, P01, BP01, xT, BxT, xsrc[tsl, :])
            fw.dma(fw.sp, ao[:], C.aoT.rearrange("(c p) t -> p c t", p=128)[:, :, tsl], reads=[C.BaoT], writes=[Bao])
            fw.dma(fw.sp, ot[:], C.oT.rearrange("(c p) t -> p c t", p=128)[:, :, tsl], reads=[C.BoT], writes=[Bot])
            for gi, (dst, Bdst) in enumerate(((sa, Bsa), (sb_, Bsb))):
                for hb in range(2):
                    for c in range(8):
                        fw.op(fw.pe, lambda e: e.matmul(P23[:, hb * 512:(hb + 1) * 512], xT[:, c, :],
                                                        wgt[:, c, gi * 1024 + hb * 512:gi * 1024 + (hb + 1) * 512],
                                                        start=(c == 0), stop=(c == 7)), reads=[Bwgt, BxT], writes=[BP23])
                fw.op(fw.act, lambda e: e.activation(out=dst[:], in_=P23[:], func=AF.Sigmoid), reads=[BP23], writes=[Bdst])
            for hb in range(2):
                for c in range(4):
                    fw.op(fw.pe, lambda e: e.matmul(P45[:, hb * 512:(hb + 1) * 512], ao[:, c, :], pa[:, c, hb * 512:(hb + 1) * 512],
                                                    start=(c == 0), stop=(c == 3)), reads=[Bao, Bpa], writes=[BP45])
            for hb in range(2):
                for c in range(8):
                    fw.op(fw.pe, lambda e: e.matmul(P67[:, hb * 512:(hb + 1) * 512], ot[:, c, :], pgl[:, c, hb * 512:(hb + 1) * 512],
                                                    start=(c == 0), stop=(c == 7)), reads=[Bot, Bpgl], writes=[BP67])
            fw.op(fw.dve, lambda e: e.tensor_tensor(out=sa[:], in0=P45[:], in1=sa[:], op=ALU.mult), reads=[BP45, Bsa], writes=[Bsa])
            fw.op(fw.dve, lambda e: e.tensor_tensor(out=sb_[:], in0=P67[:], in1=sb_[:], op=ALU.mult), reads=[BP67, Bsb], writes=[Bsb])
            fw.op(fw.dve, lambda e: e.tensor_tensor(out=mg[:], in0=sa[:], in1=sb_[:], op=ALU.add), reads=[Bsa, Bsb], writes=[Bmg])
            for c in range(8):
                fw.op(fw.pe, lambda e: e.transpose(P01[:, c * 128:(c + 1) * 128], mg[:, c * 128:(c + 1) * 128], C.identf[:]),
                      reads=[Bmg, C.Bidentf], writes=[BP01])
            fw.op(fw.act, lambda e: e.copy(mT[:].rearrange("p c t -> p (c t)"), P01[:]), reads=[BP01], writes=[BmT])
            for hb in range(2):
                for c in range(8):
                    fw.op(fw.pe, lambda e: e.matmul(P45[:, hb * 512:(hb + 1) * 512], mT[:, c, :], wo[:, c, hb * 512:(hb + 1) * 512],
                                                    start=(c == 0), stop=(c == 7)), reads=[BmT, Bwo], writes=[BP45])
            fw.op(fw.dve, lambda e: e.tensor_tensor(out=z[:], in0=P45[:], in1=bob[:], op=ALU.add), reads=[BP45, Bbob], writes=[Bz])
            fw.op(fw.dve, lambda e: e.scalar_tensor_tensor(out=z[:], in0=xt[:], scalar=DN_ALPHA, in1=z[:], op0=ALU.mult, op1=ALU.add),
                  reads=[Bxt, Bz], writes=[Bz])
            o, Bo = OUT[s]
            layer_norm_tile(C, z, Bz, o, Bo, st6, mv, Bst, lg, Blg, lb, Blb)
            fw.dma(fw.sp, xdst[tsl, :], o[:], reads=[Bo], writes=[C.BxsB])
        fw.barrier()


def phase5a(C, l, xsrc):
    fw, nc = C.fw, C.nc
    S = C.S
    TB = 512
    NB = S // TB
    NCH = 2 * DFF // 128
    with ExitStack() as es:
        WU = alloc(es, nc, "p5wu", [128, 8, 2 * DFF], BF16)
        wu, Bwu = WU
        load_weight(C, WU, C.w_up[l], 8)
        cw, Bcw = alloc(es, nc, "p5cw", [128, NCH, 3], F32)
        cb, Bcb = alloc(es, nc, "p5cb", [128, NCH], F32)
        for k in range(3):
            fw.dma(fw.sp, cw[:, :, k], C.conv_w[l][k].rearrange("(c p) -> p c", p=128), writes=[Bcw], allow_slow_non_contiguous=True)
        fw.dma(fw.sp, cb[:], C.conv_b[l].rearrange("(c p) -> p c", p=128), writes=[Bcb], allow_slow_non_contiguous=True)
        carry, Bcarry = alloc(es, nc, "p5carry", [128, NCH, 2], F32)
        fw.op(fw.pool, lambda e: e.memset(carry[:], 0.0), writes=[Bcarry])
        XT = alloc(es, nc, "p5xt", [128, 1024], F32, n=2)
        xT, BxT = alloc(es, nc, "p5xT", [128, 8, TB], BF16)
        UB = alloc(es, nc, "p5ub", [128, TB + 2], F32, n=2)
        CA = alloc(es, nc, "p5ca", [128, TB], F32, n=2)
        CBb = alloc(es, nc, "p5cbb", [128, TB], F32, n=2)
        T1 = alloc(es, nc, "p5t1", [128, TB], F32, n=2)
        GO = alloc(es, nc, "p5go", [128, TB], BF16, n=2)
        P01, BP01 = alloc(es, nc, "p5P01", [128, 1024], F32, psum=True)
        PU = alloc(es, nc, "p5pu", [128, 512], F32, n=4, psum=True)
        cu = 0
        c_gelu = 2.0 * math.sqrt(2.0 / math.pi)
        for b in range(NB):
            for q in range(TB // 128):
                xt, Bxt = XT[q % 2]
                i = b * (TB // 128) + q
                fw.dma(fw.sp, xt[:], xsrc[i * 128:(i + 1) * 128, :], writes=[Bxt])
                for c in range(8):
                    fw.op(fw.pe, lambda e: e.transpose(P01[:, c * 128:(c + 1) * 128], xt[:, c * 128:(c + 1) * 128], C.identf[:]),
                          reads=[Bxt, C.Bidentf], writes=[BP01])
                fw.op(fw.act, lambda e: e.copy(xT[:, :, q * 128:(q + 1) * 128], P01[:].rearrange("p (c t) -> p c t", t=128)),
                      reads=[BP01], writes=[BxT])
            for fc in range(NCH // 2):
                outs = []
                for half, (CC, chunk) in enumerate(((CA, fc), (CBb, fc + NCH // 2))):
                    pu, Bpu = PU[cu % 4]
                    ub, Bub = UB[cu % 2]
                    cc, Bcc = CC[fc % 2]
                    cu += 1
                    for c in range(8):
                        fw.op(fw.pe, lambda e: e.matmul(pu[:], wu[:, c, chunk * 128:(chunk + 1) * 128], xT[:, c, :],
                                                        start=(c == 0), stop=(c == 7)), reads=[Bwu, BxT], writes=[Bpu])
                    fw.op(fw.act, lambda e: e.copy(ub[:, 2:TB + 2], pu[:]), reads=[Bpu], writes=[Bub])
                    fw.op(fw.pool, lambda e: e.tensor_copy(out=ub[:, 0:2], in_=carry[:, chunk, :]), reads=[Bcarry], writes=[Bub])
                    fw.op(fw.pool, lambda e: e.tensor_copy(out=carry[:, chunk, :], in_=ub[:, TB:TB + 2]), reads=[Bub], writes=[Bcarry])
                    fw.op(fw.act, lambda e: e.activation(out=cc[:], in_=ub[:, 2:TB + 2], func=AF.Identity, scale=cw[:, chunk, 2:3],
                                                         bias=cb[:, chunk:chunk + 1]), reads=[Bub, Bcw, Bcb], writes=[Bcc])
                    fw.op(fw.dve, lambda e: e.scalar_tensor_tensor(out=cc[:], in0=ub[:, 1:TB + 1], scalar=cw[:, chunk, 1:2], in1=cc[:],
                                                                   op0=ALU.mult, op1=ALU.add), reads=[Bub, Bcw, Bcc], writes=[Bcc])
                    fw.op(fw.dve, lambda e: e.scalar_tensor_tensor(out=cc[:], in0=ub[:, 0:TB], scalar=cw[:, chunk, 0:1], in1=cc[:],
                                                                   op0=ALU.mult, op1=ALU.add), reads=[Bub, Bcw, Bcc], writes=[Bcc])
                    outs.append((cc, Bcc))
                (a, Ba), (bb, Bbb) = outs
                go, Bgo = GO[fc % 2]
                t1, Bt1 = T1[fc % 2]
                fw.op(fw.pool, lambda e: e.tensor_tensor(out=t1[:], in0=a[:], in1=a[:], op=ALU.mult), reads=[Ba], writes=[Bt1])
                fw.op(fw.dve, lambda e: e.tensor_scalar(out=t1[:], in0=t1[:], scalar1=0.044715, scalar2=1.0, op0=ALU.mult, op1=ALU.add),
                      reads=[Bt1], writes=[Bt1])
                fw.op(fw.dve, lambda e: e.tensor_tensor(out=t1[:], in0=t1[:], in1=a[:], op=ALU.mult), reads=[Bt1, Ba], writes=[Bt1])
                fw.op(fw.act, lambda e: e.activation(out=t1[:], in_=t1[:], func=AF.Sigmoid, scale=c_gelu), reads=[Bt1], writes=[Bt1])
                fw.op(fw.dve, lambda e: e.tensor_tensor(out=t1[:], in0=t1[:], in1=a[:], op=ALU.mult), reads=[Bt1, Ba], writes=[Bt1])
                fw.op(fw.pool, lambda e: e.tensor_tensor(out=go[:], in0=t1[:], in1=bb[:], op=ALU.mult), reads=[Bt1, Bbb], writes=[Bgo])
                fw.dma(fw.sp, C.gT[b * 4:(b + 1) * 4, :, fc, :].rearrange("q p t -> p q t"), go[:].rearrange("p (q t) -> p q t", t=128),
                       reads=[Bgo], writes=[C.BgT])
        fw.barrier()


def phase5b(C, l, xsrc, xdst, Bdst):
    fw, nc = C.fw, C.nc
    NT = C.NT
    NF = DFF // 128
    with ExitStack() as es:
        WD = alloc(es, nc, "p6wd", [128, NF, 1024], BF16)
        wd, Bwd = WD
        load_weight(C, WD, C.w_down[l], NF)
        lg, Blg = bcast_load(C, es, "p6lg", C.ln2_g[l])
        lb, Blb = bcast_load(C, es, "p6lb", C.ln2_b[l])
        XT = alloc(es, nc, "p6xt", [128, 1024], F32, n=2)
        GT = alloc(es, nc, "p6gt", [128, NF, 128], BF16, n=2)
        ZZ = alloc(es, nc, "p6z", [128, 1024], F32, n=2)
        OUT = alloc(es, nc, "p6out", [128, 1024], F32, n=2)
        ST6 = alloc(es, nc, "p6st6", [128, 2, 6], F32, n=2)
        MV = [es.enter_context(nc.sbuf_tensor(uname("p6mv"), [128, 4], F32)) for _ in range(2)]
        PY = alloc(es, nc, "p6py", [128, 1024], F32, n=2, psum=True)
        for i in range(NT):
            s = i % 2
            xt, Bxt = XT[s]
            gt, Bgt = GT[s]
            py, Bpy = PY[s]
            z, Bz = ZZ[s]
            st6, Bst = ST6[s]
            mv = MV[s]
            tsl = slice(i * 128, (i + 1) * 128)
            fw.dma(fw.sp, xt[:], xsrc[tsl, :], writes=[Bxt])
            fw.dma(fw.sp, gt[:], C.gT[i], reads=[C.BgT], writes=[Bgt])
            for hb in range(2):
                for c in range(NF):
                    fw.op(fw.pe, lambda e: e.matmul(py[:, hb * 512:(hb + 1) * 512], gt[:, c, :], wd[:, c, hb * 512:(hb + 1) * 512],
                                                    start=(c == 0), stop=(c == NF - 1)), reads=[Bgt, Bwd], writes=[Bpy])
            fw.op(fw.dve, lambda e: e.scalar_tensor_tensor(out=z[:], in0=xt[:], scalar=DN_ALPHA, in1=py[:], op0=ALU.mult, op1=ALU.add),
                  reads=[Bxt, Bpy], writes=[Bz])
            o, Bo = OUT[s]
            layer_norm_tile(C, z, Bz, o, Bo, st6, mv, Bst, lg, Blg, lb, Blb)
            fw.dma(fw.sp, xdst[tsl, :], o[:], reads=[Bo], writes=[Bdst])
        fw.barrier()


def build(S=8192, depth=DEPTH, stop_after=None, dbg=False, only=None):
    nc = bass.Bass("TRN2", target_bir_lowering=False)
    C = Ctx()
    C.nc = nc
    C.S = S
    C.NT = S // 128
    NT = C.NT

    def din(name, shape, dt=F32):
        return nc.dram_tensor(name, shape, dt, kind="ExternalInput").ap()

    def dscr(name, shape, dt):
        return nc.dram_tensor(name, shape, dt, kind="Internal").ap()
    C.x = din("x", [S, D])
    C.pos = din("pos", [128, NT], I32)
    C.w_in = din("w_in", [DEPTH, D, INW])
    C.gla_w_gate = din("gla_w_gate", [DEPTH, 16, 512])
    C.gla_b_gate = din("gla_b_gate", [DEPTH, 512])
    C.gla_norm_g = din("gla_norm_g", [DEPTH, 1024])
    C.p_attn = din("p_attn", [DEPTH, 512, D])
    C.p_gla = din("p_gla", [DEPTH, 1024, D])
    C.w_mix_out = din("w_mix_out", [DEPTH, D, D])
    C.b_mix_out = din("b_mix_out", [DEPTH, D])
    C.ln1_g = din("ln1_g", [DEPTH, D])
    C.ln1_b = din("ln1_b", [DEPTH, D])
    C.w_up = din("w_up", [DEPTH, D, 2 * DFF])
    C.conv_w = din("conv_w", [DEPTH, 3, 2 * DFF])
    C.conv_b = din("conv_b", [DEPTH, 2 * DFF])
    C.w_down = din("w_down", [DEPTH, DFF, D])
    C.ln2_g = din("ln2_g", [DEPTH, D])
    C.ln2_b = din("ln2_b", [DEPTH, D])
    C.out = nc.dram_tensor("out", [S, D], F32, kind="ExternalOutput").ap()
    if dbg:
        C.dbg_ao = nc.dram_tensor("dbg_ao", [512, S], BF16, kind="ExternalOutput").ap()
    C.xsA = dscr("xsA", [S, D], F32)
    C.xsB = dscr("xsB", [S, D], F32)
    C.qiT = dscr("qiT", [1024, S], BF16)
    C.kT = dscr("kT", [512, S], BF16)
    C.ikT = dscr("ikT", [64, S], BF16)
    C.va = dscr("va", [S, 520], BF16)
    C.iw = dscr("iw", [S, 8], F32)
    C.aoT = C.dbg_ao if dbg else dscr("aoT", [512, S], BF16)
    C.gT = dscr("gT", [NT, 128, DFF // 128, 128], BF16)
    C.oT = dscr("oT", [1024, S], BF16)
    for n in ["xsA", "xsB", "qiT", "kT", "ikT", "va", "iw", "aoT", "gT", "out", "oT"]:
        setattr(C, "B" + n, Buf(n))
    with ExitStack() as es:
        C.es = es
        C.fw = FW(nc, es)
        es.enter_context(nc.Block())
        setup_consts(C)
        for l in range(depth):
            xsrc = C.x if l == 0 else C.xsA
            last = (l == depth - 1)
            want = lambda p: (only is None or p in only)
            if want("1"):
                phase1(C, l, xsrc)
            if want("2"):
                phase2(C, l)
            if stop_after == "p2":
                break
            if want("3"):
                phase3(C, l, xsrc)
            if want("4"):
                phase4(C, l, xsrc, C.out if stop_after == "p4" else C.xsB)
            if stop_after == "p4":
                break
            if want("a"):
                phase5a(C, l, C.xsB)
            if stop_after == "p5a":
                break
            if want("b"):
                phase5b(C, l, C.xsB, C.out if last else C.xsA, C.Bout if last else C.BxsA)
        C.fw.barrier()
        print("ninstr", C.fw.ninstr)
    return nc


_WNAMES = ["w_in", "gla_w_gate", "gla_b_gate", "gla_norm_g", "p_attn", "p_gla", "w_mix_out", "b_mix_out",
           "ln1_g", "ln1_b", "w_up", "conv_w", "conv_b", "w_down", "ln2_g", "ln2_b"]


def kernel(**inputs):
    x = np.asarray(inputs["x"])
    pos = np.asarray(inputs["positions"])
    B, S, _ = x.shape
    nc = build(S=S, depth=DEPTH)
    wts = {k: np.ascontiguousarray(np.asarray(inputs[k], dtype=np.float32)) for k in _WNAMES}
    in_maps = []
    for b in range(B):
        m = dict(wts)
        m["x"] = np.ascontiguousarray(x[b], dtype=np.float32)
        m["pos"] = np.ascontiguousarray(pos[b].astype(np.int32).reshape(S // 128, 128).T)
        in_maps.append(m)
    res = run_bass_kernel_spmd(nc, in_maps, core_ids=list(range(B)))
    return np.stack([np.asarray(r["out"]) for r in res.results], axis=0).astype(np.float32)
```
